# Optimizing a Trainium2 kernel written in Bass

```python
import math
import jax, jax.numpy as jnp
from jax import lax
import numpy as np

D_MODEL = 4096
BATCH = 16
SEQ = 256
DEPTH = 1
DEC_BATCH = 2
DEC_SEQ = 4096
PAST_LEN = 256

GRID_W = 64
N_DIR = 2
W_S5 = D_MODEL // 2
S5_GROUP = 16
S5_GROUPS = W_S5 // S5_GROUP
S5_STATE = 64
W_LRU = D_MODEL // 2
LRU_HEADS = 16
LRU_BLOCK = W_LRU // LRU_HEADS
LRU_C = 8.0
CONV_W = 4
CONV_LEFT = 2
IN_COLS = 2 * W_S5 + 2 * W_LRU + 2 * D_MODEL
EPS = 1e-6

kernel_name = "hybrid_s5_rglru_diffusion_step"

F32 = jnp.float32


def _rmsnorm(x, g):
    xf = x.astype(F32)
    y = xf * lax.rsqrt(jnp.mean(xf * xf, axis=-1, keepdims=True) + EPS) * g.astype(F32)
    return y.astype(x.dtype)


def _cplx(re, im):
    return lax.complex(re.astype(F32), im.astype(F32))


def _combine(left, right):
    a_l, b_l = left
    a_r, b_r = right
    return a_r * a_l, a_r * b_l + b_r


def _linear_scan(a, b, h0, reverse):
    a_cum, h = lax.associative_scan(_combine, (a, b), axis=1, reverse=reverse)
    return h + a_cum * h0[:, None]


def _centred_conv(x, w, b):
    n = x.shape[-2]
    pad = [(0, 0)] * (x.ndim - 2) + [(CONV_LEFT, CONV_W - 1 - CONV_LEFT), (0, 0)]
    xp = jnp.pad(x, pad)
    out = b.astype(F32)
    for k in range(CONV_W):
        out = out + xp[..., k:k + n, :].astype(F32) * w[k].astype(F32)
    return out


def _s5_branch(u, h0_re, h0_im, p):
    bsz, n, _ = u.shape
    uf = u.astype(F32).reshape(bsz, n, S5_GROUPS, S5_GROUP)
    uc = uf.astype(jnp.complex64)
    y = uf * p["s5_d"].astype(F32).reshape(S5_GROUPS, S5_GROUP)
    finals = []
    for k, rev in enumerate((False, True)):
        lam = _cplx(p["s5_lam_re"][k], p["s5_lam_im"][k])
        step = jnp.exp(p["s5_log_step"][k].astype(F32))[:, None]
        lam_bar = jnp.exp(lam * step)
        b_bar = ((lam_bar - 1.0) / lam)[..., None] * _cplx(p["s5_b_re"][k], p["s5_b_im"][k])
        bu = jnp.einsum("blgk,gpk->blgp", uc, b_bar)
        h = _linear_scan(jnp.broadcast_to(lam_bar, bu.shape), bu,
                         _cplx(h0_re[:, k], h0_im[:, k]), rev)
        y = y + jnp.real(jnp.einsum("blgp,gkp->blgk", h,
                                    _cplx(p["s5_c_re"][k], p["s5_c_im"][k])))
        finals.append(h[:, 0] if rev else h[:, -1])
    y = jax.nn.gelu(y.reshape(bsz, n, W_S5))
    y = y * jax.nn.sigmoid(y @ p["s5_w_glu"].astype(F32) + p["s5_b_glu"].astype(F32))
    hf = jnp.stack(finals, axis=1)
    return y, jnp.real(hf), jnp.imag(hf)


def _rglru_branch(xf, h0, p):
    bsz, n, _ = xf.shape
    xh = xf.reshape(bsz, n, LRU_HEADS, LRU_BLOCK)
    outs, finals = [], []
    for k, rev in enumerate((False, True)):
        r = jax.nn.sigmoid(jnp.einsum("blhi,hij->blhj", xh, p["lru_w_r"][k].astype(F32)).reshape(bsz, n, W_LRU)
                           + p["lru_b_r"][k].astype(F32))
        i = jax.nn.sigmoid(jnp.einsum("blhi,hij->blhj", xh, p["lru_w_i"][k].astype(F32)).reshape(bsz, n, W_LRU)
                           + p["lru_b_i"][k].astype(F32))
        log_a = -LRU_C * r * jax.nn.softplus(-p["lru_lam"][k].astype(F32))
        b = jnp.sqrt(-jnp.expm1(2.0 * log_a)) * (i * xf)
        h = _linear_scan(jnp.exp(log_a), b, h0[:, k].astype(F32), rev)
        outs.append(h)
        finals.append(h[:, 0] if rev else h[:, -1])
    return outs[0] + outs[1], jnp.stack(finals, axis=1)


def _trunk_layer(x, cond, s5_h0_re, s5_h0_im, lru_h0, on_grid, p):
    mod = (jax.nn.silu(cond.astype(F32)) @ p["w_ada"].astype(F32) + p["b_ada"].astype(F32))[:, None, :]
    shift, scale, gate = jnp.split(mod, 3, axis=-1)
    h = _rmsnorm(x, p["norm_g"]).astype(F32) * (1.0 + scale) + shift
    proj = h @ p["w_in"].astype(F32)
    o1 = W_S5
    o2 = o1 + W_S5
    o3 = o2 + W_LRU
    o4 = o3 + W_LRU
    o5 = o4 + D_MODEL
    u_a, z_a, u_b, z_b, g_a, g_b = jnp.split(proj, [o1, o2, o3, o4, o5], axis=-1)
    y_a, s5_re, s5_im = _s5_branch(u_a, s5_h0_re, s5_h0_im, p)
    bsz, n, _ = u_b.shape
    if on_grid:
        rows = n // GRID_W
        xb = _centred_conv(u_b.reshape(bsz, rows, GRID_W, W_LRU), p["lru_conv_w"], p["lru_conv_b"]).reshape(bsz, n, W_LRU)
    else:
        xb = _centred_conv(u_b, p["lru_conv_w"], p["lru_conv_b"])
    y_b, lru_fin = _rglru_branch(xb, lru_h0, p)
    y_a = (y_a * jax.nn.silu(z_a)) @ p["w_out_s5"].astype(F32)
    y_b = (y_b * jax.nn.silu(z_b)) @ p["w_out_lru"].astype(F32)
    merged = jax.nn.sigmoid(g_a) * y_a + jax.nn.sigmoid(g_b) * y_b
    out = merged @ p["w_out"].astype(F32)
    x_new = (x.astype(F32) + gate * out).astype(x.dtype)
    return x_new, s5_re, s5_im, lru_fin


def setup_inputs(seed: int = 0) -> dict:
    key = jax.random.key(seed)
    ks = jax.random.split(key, 40)
    nrm = lambda k, shape, s: jax.random.normal(k, shape, F32) * s
    G, P, K = S5_GROUPS, S5_STATE, S5_GROUP
    lam_re = -0.5 * jnp.exp(nrm(ks[10], (DEPTH, N_DIR, G, P), 0.05))
    lam_im = jnp.pi * jnp.arange(P, dtype=F32) + nrm(ks[11], (DEPTH, N_DIR, G, P), 0.05)
    log_step = jax.random.uniform(ks[12], (DEPTH, N_DIR, G), F32, math.log(1e-3), math.log(1e-1))
    a_c = jax.random.uniform(ks[13], (DEPTH, N_DIR, W_LRU), F32, 0.9, 0.999)
    sig = a_c ** (1.0 / LRU_C)
    lru_lam = jnp.log(sig) - jnp.log1p(-sig)
    return {
        "x_prompt": nrm(ks[0], (BATCH, SEQ, D_MODEL), 1.0),
        "x_sample": nrm(ks[1], (DEC_BATCH, DEC_SEQ, D_MODEL), 1.0),
        "state_s5_re": nrm(ks[2], (DEC_BATCH, DEPTH, N_DIR, G, P), 0.5),
        "state_s5_im": nrm(ks[3], (DEC_BATCH, DEPTH, N_DIR, G, P), 0.5),
        "state_lru": nrm(ks[4], (DEC_BATCH, DEPTH, N_DIR, W_LRU), 0.5),
        "c": nrm(ks[5], (DEC_BATCH, D_MODEL), 1.0),
        "c_ctx": nrm(ks[6], (D_MODEL,), 1.0),
        "norm_g": 1.0 + nrm(ks[7], (DEPTH, D_MODEL), 0.02),
        "w_ada": nrm(ks[8], (DEPTH, D_MODEL, 3 * D_MODEL), 0.5 * D_MODEL ** -0.5),
        "b_ada": nrm(ks[9], (DEPTH, 3 * D_MODEL), 0.01),
        "w_in": nrm(ks[14], (DEPTH, D_MODEL, IN_COLS), D_MODEL ** -0.5),
        "s5_lam_re": lam_re,
        "s5_lam_im": lam_im,
        "s5_log_step": log_step,
        "s5_b_re": nrm(ks[15], (DEPTH, N_DIR, G, P, K), (2.0 * K) ** -0.5),
        "s5_b_im": nrm(ks[16], (DEPTH, N_DIR, G, P, K), (2.0 * K) ** -0.5),
        "s5_c_re": nrm(ks[17], (DEPTH, N_DIR, G, K, P), (2.0 * P) ** -0.5),
        "s5_c_im": nrm(ks[18], (DEPTH, N_DIR, G, K, P), (2.0 * P) ** -0.5),
        "s5_d": nrm(ks[19], (DEPTH, W_S5), 1.0),
        "s5_w_glu": nrm(ks[20], (DEPTH, W_S5, W_S5), W_S5 ** -0.5),
        "s5_b_glu": nrm(ks[21], (DEPTH, W_S5), 0.01),
        "lru_conv_w": nrm(ks[22], (DEPTH, CONV_W, W_LRU), CONV_W ** -0.5),
        "lru_conv_b": nrm(ks[23], (DEPTH, W_LRU), 0.01),
        "lru_w_r": nrm(ks[24], (DEPTH, N_DIR, LRU_HEADS, LRU_BLOCK, LRU_BLOCK), LRU_BLOCK ** -0.5),
        "lru_b_r": nrm(ks[25], (DEPTH, N_DIR, W_LRU), 0.01),
        "lru_w_i": nrm(ks[26], (DEPTH, N_DIR, LRU_HEADS, LRU_BLOCK, LRU_BLOCK), LRU_BLOCK ** -0.5),
        "lru_b_i": nrm(ks[27], (DEPTH, N_DIR, W_LRU), 0.01),
        "lru_lam": lru_lam,
        "w_out_s5": nrm(ks[28], (DEPTH, W_S5, D_MODEL), W_S5 ** -0.5),
        "w_out_lru": nrm(ks[29], (DEPTH, W_LRU, D_MODEL), W_LRU ** -0.5),
        "w_out": nrm(ks[30], (DEPTH, D_MODEL, D_MODEL), D_MODEL ** -0.5),
        "final_g": 1.0 + nrm(ks[31], (D_MODEL,), 0.02),
    }


def reference(x_prompt, x_sample, state_s5_re, state_s5_im, state_lru, c, c_ctx,
              norm_g, w_ada, b_ada, w_in, s5_lam_re, s5_lam_im, s5_log_step,
              s5_b_re, s5_b_im, s5_c_re, s5_c_im, s5_d, s5_w_glu, s5_b_glu,
              lru_conv_w, lru_conv_b, lru_w_r, lru_b_r, lru_w_i, lru_b_i, lru_lam,
              w_out_s5, w_out_lru, w_out, final_g):
    ctx = x_prompt
    lat = x_sample
    bsz = x_prompt.shape[0]
    zero_s5 = jnp.zeros((bsz, N_DIR, S5_GROUPS, S5_STATE), F32)
    zero_lru = jnp.zeros((bsz, N_DIR, W_LRU), F32)
    new_re, new_im, new_lru = [], [], []
    for l in range(DEPTH):
        p = dict(norm_g=norm_g[l], w_ada=w_ada[l], b_ada=b_ada[l], w_in=w_in[l],
                 s5_lam_re=s5_lam_re[l], s5_lam_im=s5_lam_im[l], s5_log_step=s5_log_step[l],
                 s5_b_re=s5_b_re[l], s5_b_im=s5_b_im[l], s5_c_re=s5_c_re[l], s5_c_im=s5_c_im[l],
                 s5_d=s5_d[l], s5_w_glu=s5_w_glu[l], s5_b_glu=s5_b_glu[l],
                 lru_conv_w=lru_conv_w[l], lru_conv_b=lru_conv_b[l],
                 lru_w_r=lru_w_r[l], lru_b_r=lru_b_r[l], lru_w_i=lru_w_i[l], lru_b_i=lru_b_i[l],
                 lru_lam=lru_lam[l], w_out_s5=w_out_s5[l], w_out_lru=w_out_lru[l], w_out=w_out[l])
        ctx, f_re, f_im, f_lru = _trunk_layer(ctx, c_ctx[None, :], zero_s5, zero_s5, zero_lru, False, p)
        new_re.append(f_re)
        new_im.append(f_im)
        new_lru.append(f_lru)
        lat, _, _, _ = _trunk_layer(lat, c, state_s5_re[:, l], state_s5_im[:, l], state_lru[:, l], True, p)
    y_prompt = _rmsnorm(ctx, final_g)
    y_sample = _rmsnorm(lat, final_g)
    return (y_prompt, y_sample, jnp.stack(new_re, axis=1), jnp.stack(new_im, axis=1), jnp.stack(new_lru, axis=1))
```

```python
import numpy as np
from contextlib import ExitStack
import concourse.bass as bass
import concourse.mybir as mybir
from concourse.bass_utils import run_bass_kernel_spmd

F32 = mybir.dt.float32
BF16 = mybir.dt.bfloat16
I32 = mybir.dt.int32
AF = mybir.ActivationFunctionType
ALU = mybir.AluOpType
ITEM = {F32: 4, BF16: 2, I32: 4}
ENGS = ["tensor", "vector", "scalar", "gpsimd", "sync"]

D = 4096
NDK = 32
EPS = 1e-6
NCORES = 8
LRU_POLY = False

CONST_OFF, CONST_SZ = 0, 16384
R1_OFF, R1_SZ = 16384, 65536
R2_OFF, R2_SZ = 81920, 32768
R3_OFF, R3_SZ = 114688, 32768
WP_OFF, WP_SZ = 147456, 32768
SCR_OFF, SCR_SZ = 180224, 24576
SB_TOTAL = 204800
PS_TOTAL = 16384
CELL = 64


class V:
    def __init__(s, h, space, pstep, item, off, dims, p0=0, pn=128, dkey=None):
        s.h, s.space, s.pstep, s.item, s.off, s.dims, s.p0, s.pn, s.dkey = h, space, pstep, item, off, list(dims), p0, pn, dkey

    def _new(s, off=None, dims=None, p0=None, pn=None, dkey=None):
        return V(s.h, s.space, s.pstep, s.item, s.off if off is None else off, s.dims if dims is None else dims,
                 s.p0 if p0 is None else p0, s.pn if pn is None else pn, s.dkey if dkey is None else dkey)

    def ap(s):
        return bass.AP(s.h, s.p0 * s.pstep + s.off, [[s.pstep, s.pn]] + [[a, b] for a, b in s.dims])

    def __getitem__(s, idx):
        if not isinstance(idx, tuple):
            idx = (idx,)
        off = s.off
        dims = []
        for i, (st, c) in enumerate(s.dims):
            if i < len(idx):
                ix = idx[i]
                if isinstance(ix, int):
                    assert 0 <= ix < c, (ix, c)
                    off += st * ix
                else:
                    a, b, step = ix.indices(c)
                    n = len(range(a, b, step))
                    assert n > 0
                    off += st * a
                    dims.append((st * step, n))
            else:
                dims.append((st, c))
        return s._new(off=off, dims=dims)

    def P(s, p0, pn):
        return s._new(p0=s.p0 + p0, pn=pn)

    def T(s, *perm):
        return s._new(dims=[s.dims[i] for i in perm])

    def bc(s, axis, n):
        d = list(s.dims)
        d.insert(axis, (0, n))
        return s._new(dims=d)

    def merge(s):
        d = s.dims
        tot = 1
        for a, b in d:
            tot *= b
        st = d[-1][0]
        exp = st
        for a, b in reversed(d):
            assert a == exp, ("not mergeable", d)
            exp *= b
        return s._new(dims=[(st, tot)])

    def key(s, k):
        return s._new(dkey=k)

    @property
    def shape(s):
        return [c for _, c in s.dims]

    def span(s):
        lo = hi = s.off
        for st, c in s.dims:
            if st >= 0:
                hi += st * (c - 1)
            else:
                lo += st * (c - 1)
        return lo * s.item, (hi + 1) * s.item


class Track:
    def __init__(s, ncell):
        s.n = ncell
        s.lastw = np.zeros(ncell, np.int64)
        s.reads = {}

    def deps(s, c0, c1, write):
        out = set(np.unique(s.lastw[c0:c1]).tolist())
        if write:
            for sid, arr in s.reads.items():
                m = int(arr[c0:c1].max())
                if m:
                    out.add((sid << 32) | m)
        out.discard(0)
        return out

    def rec(s, c0, c1, write, sid, val):
        if write:
            s.lastw[c0:c1] = (sid << 32) | val
            for arr in s.reads.values():
                arr[c0:c1] = 0
        else:
            arr = s.reads.get(sid)
            if arr is None:
                arr = s.reads[sid] = np.zeros(s.n, np.int64)
            np.maximum(arr[c0:c1], val, out=arr[c0:c1])


class Prog:
    def __init__(s, nc, es):
        s.nc, s.es = nc, es
        s.q = {e: [] for e in ENGS}
        s.cnt = {e: 0 for e in ENGS}
        s.sems = []
        s.esem = {e: s._newsem("e_" + e) for e in ENGS}
        s.waited = {e: {} for e in ENGS}
        s.tr = {"sb": Track(SB_TOTAL // CELL), "ps": Track(PS_TOTAL // CELL)}
        s.dtr = {}
        s.lanes = {"sync": [], "gpsimd": []}
        s.lane_rr = {"sync": 0, "gpsimd": 0}
        s.nlanes = {"sync": 24, "gpsimd": 8}
        s.lane_cnt = {}
        s.ninstr = 0

    def _newsem(s, name):
        h = s.es.enter_context(s.nc.semaphore(name))
        s.sems.append(h)
        return len(s.sems) - 1

    def _cells(s, v):
        if v.space in ("sb", "ps"):
            lo, hi = v.span()
            return s.tr[v.space], lo // CELL, (hi + CELL - 1) // CELL
        k = (v.h.name, v.dkey)
        t = s.dtr.get(k)
        if t is None:
            t = s.dtr[k] = Track(1)
        return t, 0, 1

    def _deps(s, ins, outs):
        d = set()
        for v in ins:
            t, a, b = s._cells(v)
            d |= t.deps(a, b, False)
        for v in outs:
            t, a, b = s._cells(v)
            d |= t.deps(a, b, True)
        return d

    def _emit_waits(s, eng, deps):
        best = {}
        for x in deps:
            sid, val = x >> 32, x & 0xFFFFFFFF
            if eng == "tensor" and sid == s.esem["tensor"]:
                continue
            if val > best.get(sid, 0):
                best[sid] = val
        for sid, val in best.items():
            if s.waited[eng].get(sid, 0) >= val:
                continue
            s.waited[eng][sid] = val
            sem = s.sems[sid]
            s.q[eng].append(lambda E, sem=sem, val=val: E.wait_ge(sem, val))
            s.ninstr += 1

    def _rec(s, ins, outs, sid, val):
        for v in ins:
            t, a, b = s._cells(v)
            t.rec(a, b, False, sid, val)
        for v in outs:
            t, a, b = s._cells(v)
            t.rec(a, b, True, sid, val)

    def op(s, eng, fn, ins=(), outs=(), signal=True):
        ins = [v for v in ins if isinstance(v, V)]
        s._emit_waits(eng, s._deps(ins, outs))
        sid = s.esem[eng]
        seq = s.cnt[eng] + 1
        sem = s.sems[sid]
        if signal:
            s.cnt[eng] = seq
            s.q[eng].append(lambda E, fn=fn, sem=sem: fn(E).then_inc(sem, 1))
        else:
            s.q[eng].append(lambda E, fn=fn: fn(E))
        s.ninstr += 1
        s._rec(ins, outs, sid, seq)

    def dma(s, q, out, in_):
        lanes = s.lanes[q]
        idx = s.lane_rr[q] % s.nlanes[q]
        if idx >= len(lanes):
            lanes.append(s._newsem("d_%s%d" % (q, len(lanes))))
        sid = lanes[idx]
        s.lane_rr[q] += 1
        prev = s.lane_cnt.get(sid, 0)
        deps = s._deps([in_], [out])
        if prev:
            deps.add((sid << 32) | prev)
        s._emit_waits(q, deps)
        val = prev + 16
        s.lane_cnt[sid] = val
        sem = s.sems[sid]
        oa, ia = out.ap(), in_.ap()
        s.q[q].append(lambda E, oa=oa, ia=ia, sem=sem: E.dma_start(out=oa, in_=ia).then_inc(sem, 16))
        s.ninstr += 1
        s._rec([in_], [out], sid, val)

    def finish(s):
        for sid, val in s.lane_cnt.items():
            sem = s.sems[sid]
            s.q["sync"].append(lambda E, sem=sem, val=val: E.wait_ge(sem, val))
        for e in ENGS:
            if s.cnt[e]:
                sem = s.sems[s.esem[e]]
                s.q["sync"].append(lambda E, sem=sem, val=s.cnt[e]: E.wait_ge(sem, val))


class Bump:
    def __init__(s, off, size):
        s.base, s.size, s.cur = off, size, off

    def reset(s):
        s.cur = s.base

    def take(s, nbytes, align=64):
        s.cur = (s.cur + align - 1) // align * align
        o = s.cur
        s.cur += nbytes
        assert s.cur <= s.base + s.size, ("region overflow", s.base, s.size, s.cur - s.base)
        return o


def _prod(x):
    r = 1
    for a in x:
        r *= a
    return r


class WQ:
    def __init__(s, k, items):
        s.k, s.items, s.loaded, s.nxt = k, items, [], 0
        s.cur = []
        s._fill()

    def _fill(s):
        k = s.k
        while s.nxt < len(s.items):
            src, shape = s.items[s.nxt]
            nb = _prod(shape) * 2
            out = sum(b_ - a_ for a_, b_ in k.w_out)
            if out + nb > WP_SZ - 2048:
                break
            lo = 0 if k.w_cur + nb > WP_SZ else k.w_cur
            if any(not (lo + nb <= a_ or lo >= b_) for a_, b_ in k.w_out):
                break
            s.loaded.append(k.w_load(src, BF16, shape))
            s.nxt += 1

    def get(s):
        if not s.loaded:
            s._fill()
        v = s.loaded.pop(0)
        s.cur.append(v)
        return v

    def done(s, v=None):
        if v is None:
            v = s.cur.pop(0)
        else:
            s.cur.remove(v)
        s.k.w_done(v)
        s._fill()


class K:
    def __init__(s, dbg=(), stages="all", stop=None):
        s.stop = stop
        s.dbg = list(dbg)
        s.stages = stages
        s.taps = []

    def sb(s, off_bytes, dt, shape, pn=128):
        item = ITEM[dt]
        assert off_bytes % item == 0
        h = {F32: s.AR32, BF16: s.AR16, I32: s.ARI}[dt]
        dims = []
        st = 1
        for c in reversed(shape):
            dims.insert(0, (st, c))
            st *= c
        assert off_bytes + st * item <= SB_TOTAL
        return V(h, "sb", SB_TOTAL // item, item, off_bytes // item, dims, 0, pn)

    def alloc(s, bump, dt, shape, pn=128):
        return s.sb(bump.take(_prod(shape) * ITEM[dt]), dt, shape, pn)

    def psv(s, bank, dt, shape, col_bytes=0, pn=128):
        item = ITEM[dt]
        h = s.PS32 if dt == F32 else s.PS16
        dims = []
        st = 1
        for c in reversed(shape):
            dims.insert(0, (st, c))
            st *= c
        off_b = bank * 2048 + col_bytes
        assert off_b + st * item <= PS_TOTAL
        return V(h, "ps", PS_TOTAL // item, item, off_b // item, dims, 0, pn)

    def ps_alloc(s, n=1, hold=False):
        for _ in range(16):
            if s.ps_cur + n > 8:
                s.ps_cur = 0
            b = s.ps_cur
            if any((b + k) in s.ps_res for k in range(n)):
                s.ps_cur = b + 1
                continue
            s.ps_cur = b + n
            if hold:
                for k in range(n):
                    s.ps_res.add(b + k)
            return b
        raise RuntimeError("psum ring: no free bank")

    def ps_release(s, b, n=1):
        for k in range(n):
            s.ps_res.discard(b + k)

    def dram(s, name, shape, dt, kind=None):
        if kind:
            h = s.nc.dram_tensor(name, list(shape), dt, kind=kind)
        else:
            h = s.nc.dram_tensor(name, list(shape), dt)
        dims = []
        st = 1
        for c in reversed(shape):
            dims.insert(0, (st, c))
            st *= c
        v = V(h, "dram", dims[0][0], ITEM[dt], 0, dims[1:], 0, dims[0][1], dkey=None)
        return v

    def dsel(s, v, lead_shape, idx, dkey=None):
        full = [(v.pstep, v.pn)] + list(v.dims)
        off = v.off
        if not isinstance(idx, tuple):
            idx = (idx,)
        for i, ix in enumerate(idx):
            off += full[i][0] * ix
        rest = full[len(idx):]
        return V(v.h, "dram", rest[0][0], v.item, off, rest[1:], 0, rest[0][1], dkey=dkey)

    def act(s, out, in_, func, scale=None, bias=None, accum=None):
        kw = {}
        ins = [in_]
        if scale is not None:
            if isinstance(scale, V):
                ins.append(scale)
                kw["scale"] = scale.ap()
            else:
                kw["scale"] = float(scale)
        if bias is not None:
            if isinstance(bias, V):
                ins.append(bias)
                kw["bias"] = bias.ap()
            else:
                kw["bias"] = float(bias)
        oa, ia = out.ap(), in_.ap()
        outs = [out]
        if accum is not None:
            kw["accum_out"] = accum.ap()
            outs.append(accum)
        s.p.op("scalar", lambda E: E.activation(out=oa, in_=ia, func=func, **kw), ins, outs)

    def tt(s, out, a, b, op, eng="vector"):
        oa, aa, ba = out.ap(), a.ap(), b.ap()
        s.p.op(eng, lambda E: E.tensor_tensor(out=oa, in0=aa, in1=ba, op=op), [a, b], [out])

    def ts(s, out, a, s1, op0, s2=None, op1=None, eng="vector"):
        ins = [a]
        oa, aa = out.ap(), a.ap()
        cv = lambda x: x.ap() if isinstance(x, V) else (None if x is None else (x if isinstance(x, int) else float(x)))
        x1, x2 = cv(s1), cv(s2)
        if isinstance(s1, V):
            ins.append(s1)
        if isinstance(s2, V):
            ins.append(s2)
        if op1 is None:
            s.p.op(eng, lambda E: E.tensor_scalar(out=oa, in0=aa, scalar1=x1, scalar2=None, op0=op0), ins, [out])
        else:
            s.p.op(eng, lambda E: E.tensor_scalar(out=oa, in0=aa, scalar1=x1, scalar2=x2, op0=op0, op1=op1), ins, [out])

    def stt(s, out, a, sc, b, op0, op1, eng="vector"):
        ins = [a, b]
        oa, aa, ba = out.ap(), a.ap(), b.ap()
        x = sc.ap() if isinstance(sc, V) else float(sc)
        if isinstance(sc, V):
            ins.append(sc)
        s.p.op(eng, lambda E: E.scalar_tensor_tensor(out=oa, in0=aa, scalar=x, in1=ba, op0=op0, op1=op1), ins, [out])

    def cp(s, out, in_, eng="vector"):
        oa, ia = out.ap(), in_.ap()
        if eng == "scalar":
            s.p.op("scalar", lambda E: E.activation(out=oa, in_=ia, func=AF.Identity), [in_], [out])
        else:
            s.p.op(eng, lambda E: E.tensor_copy(out=oa, in_=ia), [in_], [out])

    def memset(s, out, val, eng="vector"):
        oa = out.ap()
        s.p.op(eng, lambda E: E.memset(oa, float(val)), [], [out])

    def recip(s, out, in_):
        oa, ia = out.ap(), in_.ap()
        s.p.op("vector", lambda E: E.reciprocal(out=oa, in_=ia), [in_], [out])

    def scan(s, out, d0, d1, init, op0=ALU.mult, op1=ALU.add):
        ins = [d0, d1]
        oa, a0, a1 = out.ap(), d0.ap(), d1.ap()
        x = init.ap() if isinstance(init, V) else float(init)
        if isinstance(init, V):
            ins.append(init)
        s.p.op("vector", lambda E: E.tensor_tensor_scan(out=oa, data0=a0, data1=a1, initial=x, op0=op0, op1=op1), ins, [out])

    def mm(s, out, lhsT, rhs, start, stop, signal=None):
        if signal is None:
            signal = stop
        oa, la, ra = out.ap(), lhsT.ap(), rhs.ap()
        s.p.op("tensor", lambda E: E.matmul(oa, la, ra, start=start, stop=stop), [lhsT, rhs], [out], signal=signal)

    def trp(s, out, in_, ident, signal=True):
        oa, ia, da = out.ap(), in_.ap(), ident.ap()
        s.p.op("tensor", lambda E: E.transpose(out=oa, in_=ia, identity=da), [in_, ident], [out], signal=signal)

    def dma(s, out, in_, q="sync"):
        s.p.dma(q, out, in_)

    def tap(s, name, v, dt=F32):
        if name not in s.dbg:
            return
        shape = [v.pn] + v.shape
        d = s.dram("dbg_" + name, shape, dt, kind="ExternalOutput")
        s.taps.append(("dbg_" + name, shape))
        s.dma(d, v)

    def w_reset(s):
        s.w_cur = 0
        s.w_out = []

    def w_load(s, src, dt, shape):
        nb = _prod(shape) * ITEM[dt]
        if s.w_cur + nb > WP_SZ:
            s.w_cur = 0
        lo, hi = s.w_cur, s.w_cur + nb
        for (a, b) in s.w_out:
            assert hi <= a or lo >= b, "weight ring overrun (prefetch too deep)"
        s.w_out.append((lo, hi))
        s.w_cur = hi
        v = s.sb(WP_OFF + lo, dt, shape)
        v.wr = (lo, hi)
        s.dma(v, src, q="gpsimd")
        return v

    def w_done(s, v=None):
        if v is None:
            s.w_out.pop(0)
        else:
            s.w_out.remove(v.wr)

    def build(s):
        nc = s.nc = bass.Bass("TRN2", target_bir_lowering=False)
        s.es = es = ExitStack()
        I = lambda n, sh: s.dram(n, sh, F32, kind="ExternalInput")
        s.d_xown = I("xT_own", [128, NDK, 1024])
        s.d_xoth = I("xT_oth", [3, 128, NDK, 1024])
        s.d_xp = I("xT_p", [128, NDK, 512])
        s.d_cond = I("condT", [128, NDK, 2])
        s.d_vecs = I("vecs", [128, 392])
        s.d_s5small = I("s5small", [128, 3, 128])
        s.d_s5B = I("s5B", [128, 2, 128, 16])
        s.d_s5C = I("s5C", [128, 2, 128, 16])
        s.d_s5Dm = I("s5Dm", [128, 128])
        s.d_s5h0 = I("s5h0", [128, 2, 128])
        s.d_consts = I("consts", [128, 3, 128])
        s.d_wada = I("w_ada_s", [24, 128, NDK, 512])
        s.d_win = I("w_in_s", [128, 128, NDK, 128])
        s.d_wglu = I("w_glu_s", [16, 128, 16, 128])
        s.d_wos = I("w_os_s", [32, 128, 16, 128])
        s.d_wol = I("w_ol_s", [32, 128, 16, 128])
        s.d_wo = I("w_o_s", [32, 128, NDK, 128])
        s.d_wgate = I("w_gate", [16, 128, 4, 128])
        O = lambda n, sh: s.dram(n, sh, F32, kind="ExternalOutput")
        s.d_ys = O("yT_s", [128, NDK, 1024])
        s.d_yp = O("yT_p", [128, NDK, 512])
        s.d_stlru = O("st_lru", [128, 16, 2, 2])
        s.d_sts5 = O("st_s5", [128, 2, 2, 128])
        s.d_hts = s.dram("hT_spill", [128, NDK, 1024], BF16)
        s.d_tab = {n: s.dram("tab_" + n, [16, 128, 1024], BF16) for n in ("WTr", "WTi", "M", "Or", "Oi")}
        s.d_tabAT = s.dram("tab_AT", [16, 128, 192], F32)

        s.AR32 = es.enter_context(nc.sbuf_tensor("arena", [128, SB_TOTAL // 4], F32))
        s.AR16 = s.AR32.bitcast(BF16)
        s.ARI = s.AR32.bitcast(I32)
        s.PS32 = es.enter_context(nc.psum_tensor("psum", [128, PS_TOTAL // 4], F32))
        s.PS16 = s.PS32.bitcast(BF16)
        s.p = Prog(nc, es)
        s.ps_cur = 0
        s.ps_res = set()
        s.w_reset()
        s.cb = Bump(CONST_OFF, CONST_SZ)

        s.emit()

        s.p.finish()
        with nc.Block() as block:
            for e in ENGS:
                q = s.p.q[e]

                def run(E, q=q):
                    for f in q:
                        f(E)

                getattr(block, e)(run)
        es.close()
        return nc

    def emit(s):
        st = s.stages
        s.stage_consts()
        gm = s.stage_mod()
        gt = s.stage_tables()
        next(gt)
        for i in range(24):
            next(gm)
            if i % 3 == 2 or i >= 22:
                next(gt, None)
            if i >= 8:
                next(gt, None)
        for g in (gm, gt):
            for _ in g:
                pass
        if st == "pre":
            return
        cfgP = dict(name="P", NT=512, ntb=1, nseg=2, NCs=32, L=5, ci=1, conv=(2, 256), xd=s.d_xp, yd=s.d_yp)
        cfgS = dict(name="S", NT=1024, ntb=2, nseg=1, NCs=128, L=7, ci=0, conv=(16, 64), xd=s.d_xown, yd=s.d_ys)
        if st in ("P", "all"):
            s.run_pass(cfgP, full=True)
            s.dma(s.d_stlru, s.finl)
            s.dma(s.d_sts5, s.fin5)
        if st in ("S0", "all"):
            for o in range(3):
                c0 = dict(cfgS)
                c0["xd"] = s.dsel(s.d_xoth, None, o)
                s.run_pass(c0, full=False, o=o)
            s.stage_fold()
        if st in ("S", "all"):
            if st == "S":
                s.cp(s.hin5, s.s5h0)
                s.cp(s.hinl, s.c_h0lru)
            s.run_pass(cfgS, full=True)

    def run_pass(s, cfg, full, o=None):
        NT = cfg["NT"]
        hT = s.sb(R1_OFF, BF16, [NDK, NT])
        s.stage_hT(cfg, hT, spill=(full and cfg["name"] == "S"))
        s.tap("hT_" + cfg["name"], hT, BF16)
        if s.stop == "hT":
            return
        YaT = s.sb(R2_OFF, BF16, [16, NT])
        Ya3 = s.sb(R3_OFF, BF16, [16, NT])
        Yb = s.sb(R2_OFF, BF16, [16, NT])
        if not full:
            items = [(s.dsel(s.d_win, None, 32), [NDK, 128]), (s.dsel(s.d_win, None, 0), [NDK, 128])]
            for i in range(16):
                if i + 1 < 16:
                    items.append((s.dsel(s.d_win, None, i + 1), [NDK, 128]))
                items.append((s.dsel(s.d_wgate, None, i), [4, 128]))
                if i + 1 < 16:
                    items.append((s.dsel(s.d_win, None, 32 + i + 1), [NDK, 128]))
            wq = WQ(s, items)
            g5 = s.stage_s5_sum(cfg, hT, o, wq)
            gl = s.stage_lru(cfg, hT, Yb, full, o, wq=wq, ws_off=R2_OFF)
            next(gl)
            next(g5)
            for i in range(16):
                next(g5)
                next(gl)
            for g in (g5, gl):
                for _ in g:
                    pass
            return
        for _ in s.stage_s5(cfg, hT, YaT, full, o):
            pass
        if s.stop == "s5":
            s.tap("YaT_" + cfg["name"], YaT, BF16)
            return
        s.tap("YaT_" + cfg["name"], YaT, BF16)
        s.stage_glu(cfg, hT, YaT, Ya3)
        s.tap("Ya3_" + cfg["name"], Ya3, BF16)
        if s.stop == "glu":
            return
        for _ in s.stage_lru(cfg, hT, Yb, full, o):
            pass
        s.tap("Yb_" + cfg["name"], Yb, BF16)
        if s.stop == "lru":
            return
        s.stage_merge_out(cfg, Ya3, Yb)

    def stage_hT(s, cfg, hT, spill):
        NT, ntb, ci, xd = cfg["NT"], cfg["ntb"], cfg["ci"], cfg["xd"]
        b = Bump(R2_OFF, R2_SZ)
        xc = [s.alloc(b, F32, [2, NT]) for _ in range(3)]
        tmpf = [s.alloc(b, F32, [NT]) for _ in range(2)]
        b3 = Bump(R3_OFF, R3_SZ)
        sqs = [s.alloc(b3, F32, [2, NT]) for _ in range(2)]
        rs = s.sb(SCR_OFF, F32, [NT])
        pss = [s.ps_alloc(1) for _ in range(ntb)]
        for c in range(16):
            x = xc[c % 3]
            sq = sqs[c % 2]
            s.dma(x, xd[2 * c:2 * c + 2])
            s.act(sq, x, AF.Square)
            for j in range(2):
                for tb in range(ntb):
                    s.mm(s.psv(pss[tb], F32, [512]), s.ones32, sq[j, tb * 512:(tb + 1) * 512],
                         start=(c == 0 and j == 0), stop=(c == 15 and j == 1), signal=(j == 1 and tb == ntb - 1))
        for tb in range(ntb):
            r = rs[tb * 512:(tb + 1) * 512]
            s.ts(r, s.psv(pss[tb], F32, [512]), 1.0 / D, ALU.mult, EPS, ALU.add)
            s.act(r, r, AF.Sqrt)
            s.recip(r, r)
        for c in range(16):
            x = xc[(c + 1) % 3]
            s.dma(x, xd[2 * c:2 * c + 2])
            for j in range(2):
                dk = 2 * c + j
                t = tmpf[j]
                s.stt(t, x[j], s.gs[dk, ci:ci + 1], rs, ALU.mult, ALU.mult)
                s.act(hT[dk], t, AF.Identity, bias=s.sh[dk, ci:ci + 1])
        if spill:
            s.dma(s.d_hts, hT)

    def cmac(s, dst, src, ATk, t1, t2):
        n = src.shape[0]
        Ar = ATk[0].bc(0, 2).bc(0, n)
        A2b = ATk[1:3].bc(0, n)
        t1v, t2v = t1[0:n], t2[0:n]
        s.tt(t1v, src, Ar, ALU.mult)
        s.tt(t2v, src[:, ::-1, :], A2b, ALU.mult)
        s.tt(t1v, t1v, t2v, ALU.add)
        s.tt(dst, dst, t1v, ALU.add)

    def stage_s5(s, cfg, hT, YaT, full, o, wq=None):
        NT, nseg, NCs, L = cfg["NT"], cfg["nseg"], cfg["NCs"], cfg["L"]
        NCt = NT // 8
        isS = cfg["name"] == "S"
        b3 = Bump(R3_OFF, R3_SZ)
        tabs = []
        for sl in range(2):
            t = {n: s.alloc(b3, BF16, [8, 128]) for n in ("WTr", "WTi", "M", "Or", "Oi")}
            t["AT"] = s.alloc(b3, F32, [8, 3, 8])
            tabs.append(t)
        Uc2 = s.alloc(b3, BF16, [8, 8, 16])
        X = s.alloc(b3, BF16, [8, NCt])
        Hy = s.alloc(b3, BF16, [NCt, 2, 8])
        bs = Bump(SCR_OFF, SCR_SZ)
        Hs = s.alloc(bs, F32, [nseg, NCs, 2, 8])
        t1 = s.alloc(bs, F32, [max(NCs // 2, 1), 2, 8])
        t2 = s.alloc(bs, F32, [max(NCs // 2, 1), 2, 8])
        Yc = s.alloc(bs, F32, [8, 8, 16])
        g1 = s.alloc(bs, F32, [8, NCt])
        tnames = ("WTr", "WTi", "M", "Or", "Oi") if full else ("WTr", "WTi")
        if wq is None:
            wq = WQ(s, [(s.dsel(s.d_win, None, i), [NDK, 128]) for i in range(16)])

        def A1(i):
            tb_ = tabs[i % 2]
            for n in tnames:
                s.dma(tb_[n].merge(), s.dsel(s.d_tab[n], None, i, dkey=i))
            s.dma(tb_["AT"].merge(), s.dsel(s.d_tabAT, None, i, dkey=i))
            wua = wq.get()
            pb = s.ps_alloc(2)
            pp = s.psv(pb, F32, [8, 128]).P(0, NCt)
            for s_ in range(8):
                for dk in range(NDK):
                    s.mm(pp[s_], hT[dk, s_ * NCt:(s_ + 1) * NCt], wua[dk], start=(dk == 0), stop=(dk == NDK - 1))
            wq.done(wua)
            ppv = pp._new(dims=[(128, 8), (16, 8), (1, 16)])
            Ucv = Uc2.P(0, NCt).T(1, 0, 2)
            s.act(Ucv[0:4], ppv[0:4], AF.Identity)
            s.act(Ucv[4:8], ppv[4:8], AF.Identity)

        def A2(i):
            tb_ = tabs[i % 2]
            gsl = slice(i * 8, i * 8 + 8)
            pbT = s.ps_alloc(1)
            pT = s.psv(pbT, BF16, [8, NCt])
            Ucf = Uc2._new(dims=[(128, 8), (1, 128)]).P(0, NCt)
            idb = s.identb.P(0, NCt)[0:NCt]
            for g in range(8):
                s.trp(pT[g], Ucf[g], idb)
            s.cp(X, pT)
            nbk = max(1, (8 * NCt * 4) // 2048)
            for hg in range(2):
                pbS = s.ps_alloc(nbk)
                pS = s.psv(pbS, F32, [4, 2, NCt])
                for g in range(4):
                    for comp in range(2):
                        s.mm(pS[g, comp], tb_["WTr" if comp == 0 else "WTi"][hg * 4 + g], X[hg * 4 + g], True, True)
                gq = slice(hg * 4, hg * 4 + 4)
                for comp in range(2):
                    src = pS[:, comp, :]
                    dstf = Hs[:, :, comp, gq]._new(dims=[(16, NCt), (1, 4)])
                    s.act(dstf.P(0, 64), src.T(1, 0).P(0, 64), AF.Identity)
                    for sg in range(nseg):
                        srcb = pS[:, comp, sg * NCs:(sg + 1) * NCs].T(1, 0)
                        s.cp(Hs[sg, ::-1, comp, gq].P(64, 64), srcb.P(64, 64))
            if full and isS:
                s.cmac(Hs[0, 0:1], s.hin5[:, gsl]._new(dims=[(0, 1)] + s.hin5[:, gsl].dims), tb_["AT"][0], t1, t2)
            for sg in range(nseg):
                H = Hs[sg]
                for k in range(L):
                    d = 1 << k
                    s.cmac(H[2 * d - 1::2 * d], H[d - 1::2 * d], tb_["AT"][k], t1, t2)
                if not full:
                    continue
                for k in range(L - 2, -1, -1):
                    d = 1 << k
                    s.cmac(H[3 * d - 1::2 * d], H[2 * d - 1:NCs - 1:2 * d], tb_["AT"][k], t1, t2)
            if not full:
                s.cp(s.hl5[o][:, gsl], Hs[0, NCs - 1])
            elif not isS:
                for sg in range(nseg):
                    s.cp(s.fin5[sg][:, gsl], Hs[sg, NCs - 1])

        def B(i):
            tb_ = tabs[i % 2]
            gsl = slice(i * 8, i * 8 + 8)
            for sg in range(nseg):
                base = sg * NCs
                s.act(Hy[base + 1:base + NCs].P(0, 64), Hs[sg, 0:NCs - 1].P(0, 64), AF.Identity)
                s.cp(Hy[base:base + NCs - 1].P(64, 64), Hs[sg, NCs - 2::-1].P(64, 64))
                if isS:
                    s.cp(Hy[base].P(0, 64), s.hin5[:, gsl].P(0, 64))
                    s.cp(Hy[base + NCs - 1].P(64, 64), s.hin5[:, gsl].P(64, 64))
                else:
                    s.memset(Hy[base].P(0, 64), 0.0)
                    s.memset(Hy[base + NCs - 1].P(64, 64), 0.0)
            pbY = s.ps_alloc(2)
            pY = s.psv(pbY, F32, [8, 128]).P(0, NCt)
            for g in range(8):
                s.mm(pY[g], X[g], tb_["M"][g], True, False)
                s.mm(pY[g], Hy[:, 0, g], tb_["Or"][g], False, False)
                s.mm(pY[g], Hy[:, 1, g], tb_["Oi"][g], False, True)
            pYv = pY._new(dims=[(128, 8), (16, 8), (1, 16)])
            Ycv = Yc.P(0, NCt).T(1, 0, 2)
            s.act(Ycv[0:4], pYv[0:4], AF.Identity)
            s.cp(Ycv[4:8], pYv[4:8])
            nbt = max(1, (8 * NCt * 4) // 2048)
            pb2 = s.ps_alloc(nbt)
            pY2 = s.psv(pb2, F32, [8, NCt])
            Ycf = Yc._new(dims=[(128, 8), (1, 128)]).P(0, NCt)
            idf = s.identf.P(0, NCt)[0:NCt]
            for t in range(8):
                s.trp(pY2[t], Ycf[t], idf)
            yv = YaT[i]._new(dims=[(NCt, 8), (1, NCt)])
            tper = 8 // nbt
            for hb in range(nbt):
                ts_ = slice(hb * tper, (hb + 1) * tper)
                src, gg, out = pY2[ts_], g1[ts_], yv[ts_]
                s.act(gg, src, AF.Square)
                s.ts(gg, gg, 0.044715, ALU.mult, 1.0, ALU.add)
                s.tt(gg, gg, src, ALU.mult)
                s.act(gg, gg, AF.Sigmoid, scale=1.5957691216057308)
                s.tt(out, gg, src, ALU.mult)

        for i in range(16):
            A1(i)
            if full and i > 0:
                B(i - 1)
            A2(i)
            yield
        if full:
            B(15)

    def stage_s5_sum(s, cfg, hT, o, wq):
        NT, NCs, L = cfg["NT"], cfg["NCs"], cfg["L"]
        NCt = NT // 8
        b3 = Bump(R3_OFF, R3_SZ)
        tabs = []
        for sl in range(3):
            t = {n: s.alloc(b3, BF16, [8, 128]) for n in ("WTr", "WTi")}
            t["AT"] = s.alloc(b3, F32, [8, 3, 8])
            tabs.append(t)
        Uc2s = [s.alloc(b3, BF16, [8, 8, 16]) for _ in range(2)]
        Xs = [s.alloc(b3, BF16, [8, NCt]) for _ in range(2)]
        bs = Bump(SCR_OFF, SCR_SZ)
        Hs = s.alloc(bs, F32, [1, NCs, 2, 8])
        t1 = s.alloc(bs, F32, [NCs // 2, 2, 8])
        t2 = s.alloc(bs, F32, [NCs // 2, 2, 8])

        def S1(i):
            tb_ = tabs[i % 3]
            for n in ("WTr", "WTi"):
                s.dma(tb_[n].merge(), s.dsel(s.d_tab[n], None, i, dkey=i))
            s.dma(tb_["AT"].merge(), s.dsel(s.d_tabAT, None, i, dkey=i))
            wua = wq.get()
            pb = s.ps_alloc(2)
            pp = s.psv(pb, F32, [8, 128]).P(0, NCt)
            for s_ in range(8):
                for dk in range(NDK):
                    s.mm(pp[s_], hT[dk, s_ * NCt:(s_ + 1) * NCt], wua[dk], start=(dk == 0), stop=(dk == NDK - 1))
            wq.done(wua)
            ppv = pp._new(dims=[(128, 8), (16, 8), (1, 16)])
            Ucv = Uc2s[i % 2].P(0, NCt).T(1, 0, 2)
            s.act(Ucv[0:4], ppv[0:4], AF.Identity)
            s.act(Ucv[4:8], ppv[4:8], AF.Identity)

        def S2(i):
            pbT = s.ps_alloc(1)
            pT = s.psv(pbT, BF16, [8, NCt])
            Ucf = Uc2s[i % 2]._new(dims=[(128, 8), (1, 128)]).P(0, NCt)
            idb = s.identb.P(0, NCt)[0:NCt]
            for g in range(8):
                s.trp(pT[g], Ucf[g], idb)
            s.cp(Xs[i % 2], pT)

        def S3(i):
            tb_ = tabs[i % 3]
            X = Xs[i % 2]
            gsl = slice(i * 8, i * 8 + 8)
            nbk = max(1, (8 * NCt * 4) // 2048)
            for hg in range(2):
                pbS = s.ps_alloc(nbk)
                pS = s.psv(pbS, F32, [4, 2, NCt])
                for g in range(4):
                    for comp in range(2):
                        s.mm(pS[g, comp], tb_["WTr" if comp == 0 else "WTi"][hg * 4 + g], X[hg * 4 + g], True, True)
                gq = slice(hg * 4, hg * 4 + 4)
                for comp in range(2):
                    src = pS[:, comp, :]
                    dstf = Hs[:, :, comp, gq]._new(dims=[(16, NCt), (1, 4)])
                    s.act(dstf.P(0, 64), src.T(1, 0).P(0, 64), AF.Identity)
                    s.cp(Hs[0, ::-1, comp, gq].P(64, 64), pS[:, comp, :].T(1, 0).P(64, 64))
            H = Hs[0]
            for k in range(L):
                d = 1 << k
                s.cmac(H[2 * d - 1::2 * d], H[d - 1::2 * d], tb_["AT"][k], t1, t2)
            s.cp(s.hl5[o][:, gsl], Hs[0, NCs - 1])

        S1(0)
        yield
        for i in range(16):
            if i + 1 < 16:
                S1(i + 1)
            S2(i)
            if i >= 1:
                S3(i - 1)
            yield
        S3(15)

    def stage_glu(s, cfg, hT, YaT, Ya3):
        ntb = cfg["ntb"]
        bs = Bump(SCR_OFF, SCR_SZ)
        sg = s.alloc(bs, F32, [512]); sz = s.alloc(bs, F32, [512]); tt_ = s.alloc(bs, F32, [512])
        items = []
        for j in range(16):
            items.append((s.dsel(s.d_wglu, None, j), [16, 128]))
            items.append((s.dsel(s.d_win, None, 16 + j), [NDK, 128]))
        wq = WQ(s, items)
        for j in range(16):
            wg = wq.get()
            wz = wq.get()
            for tb in range(ntb):
                tsl = slice(tb * 512, (tb + 1) * 512)
                pg = s.psv(s.ps_alloc(1), F32, [512])
                for k in range(16):
                    s.mm(pg, wg[k], YaT[k, tsl], k == 0, k == 15)
                pz = s.psv(s.ps_alloc(1), F32, [512])
                for k in range(NDK):
                    s.mm(pz, wz[k], hT[k, tsl], k == 0, k == NDK - 1)
                s.act(sg, pg, AF.Sigmoid, bias=s.c_bglu[j:j + 1])
                s.act(sz, pz, AF.Silu)
                s.tt(tt_, YaT[j, tsl], sg, ALU.mult)
                s.tt(Ya3[j, tsl], tt_, sz, ALU.mult)
            wq.done()
            wq.done()

    def stage_lru(s, cfg, hT, Yb, full, o, wq=None, ws_off=SCR_OFF):
        NT, ntb, nseg = cfg["NT"], cfg["ntb"], cfg["nseg"]
        R, Lr = cfg["conv"]
        isS = cfg["name"] == "S"
        SL = NT // nseg
        RB = 512 // Lr
        bs = Bump(ws_off, R2_SZ if ws_off == R2_OFF else SCR_SZ)
        racc = s.alloc(bs, F32, [2])
        rsum = s.alloc(bs, F32, [1])
        xb = s.alloc(bs, F32, [NT]); xbb = s.alloc(bs, BF16, [NT])
        a = s.alloc(bs, F32, [NT]); ib = s.alloc(bs, F32, [NT]); hf = s.alloc(bs, F32, [NT]); hb = s.alloc(bs, F32, [NT])
        u = hb
        both = (not full) or (not isS)
        if both:
            a1 = s.alloc(bs, F32, [NT]); ib1 = s.alloc(bs, F32, [NT])
            racc2 = s.alloc(bs, F32, [2, 2]); rsum2 = s.alloc(bs, F32, [2])
        NCt = NT // 8
        spb = 512 // NCt
        nat = lambda v, tb: v._new(off=v.off + tb * spb, dims=[(1, spb), (8, NCt)])
        if wq is None:
            items = []
            for j in range(16):
                items.append((s.dsel(s.d_win, None, 32 + j), [NDK, 128]))
                items.append((s.dsel(s.d_wgate, None, j), [4, 128]))
                if full:
                    items.append((s.dsel(s.d_win, None, 48 + j), [NDK, 128]))
            wq = WQ(s, items)
        zero_bc = s.zero1._new(dims=[(0, NT)])
        T = lambda tb: slice(tb * 512, (tb + 1) * 512)

        def proj_u(j):
            wub = wq.get()
            banks = []
            for tb in range(ntb):
                bk = s.ps_alloc(1, hold=True)
                banks.append(bk)
                pu = s.psv(bk, F32, [512])
                for k in range(NDK):
                    s.mm(pu, wub[k], hT[k, T(tb)], k == 0, k == NDK - 1)
            wq.done(wub)
            return banks

        def conv(j, banks):
            for tb in range(ntb):
                pu = s.psv(banks[tb], F32, [spb, NCt])
                s.act(nat(u, tb), pu, AF.Identity)
                s.ps_release(banks[tb])
            s.act(xb, u, AF.Identity, scale=s.c_convw[2, j:j + 1], bias=s.c_convb[j:j + 1])
            u3 = u._new(dims=[(Lr, R), (1, Lr)])
            xb3 = xb._new(dims=[(Lr, R), (1, Lr)])
            for k, off in ((0, -2), (1, -1), (3, 1)):
                lo, hi = max(0, -off), Lr - max(0, off)
                s.stt(xb3[:, lo:hi], u3[:, lo + off:hi + off], s.c_convw[k, j:j + 1], xb3[:, lo:hi], ALU.mult, ALU.add)
            s.act(xbb, xb, AF.Identity)

        def gates(j, wgt, d):
            tmpb = hf if d == 0 else hb
            prs = []
            for tb in range(ntb):
                pr = s.psv(s.ps_alloc(1), F32, [512])
                s.mm(pr, wgt[d * 2 + 0], xbb[T(tb)], True, True)
                pi = s.psv(s.ps_alloc(1), F32, [512])
                s.mm(pi, wgt[d * 2 + 1], xbb[T(tb)], True, True)
                prs.append((pr, pi))
            for tb in range(ntb):
                pr, pi = prs[tb]
                s.act(a[T(tb)], pr, AF.Sigmoid, bias=s.c_br[d, j:j + 1], accum=(None if full else racc[tb:tb + 1]))
                s.act(ib[T(tb)], pi, AF.Sigmoid, bias=s.c_bi[d, j:j + 1])
            if not full:
                if ntb == 2:
                    s.tt(rsum, racc[0:1], racc[1:2], ALU.add)
                else:
                    s.cp(rsum, racc[0:1])
                s.act(s.pll[o, d, j:j + 1], rsum, AF.Exp, scale=s.nsp8[d, j:j + 1])
            if LRU_POLY:
                for tb in range(ntb):
                    tmp = tmpb[T(tb)]
                    cO = s.pcO[d, j]
                    s.ts(tmp, a[T(tb)], cO[4:5], ALU.mult)
                    for n in (3, 2, 1, 0):
                        s.stt(tmp, tmp, cO[n:n + 1], a[T(tb)], ALU.add, ALU.mult)
                    s.ts(tmp, tmp, 0.0, ALU.max)
                    s.act(tmp, tmp, AF.Sqrt)
                    s.tt(ib[T(tb)], ib[T(tb)], xb[T(tb)], ALU.mult)
                    s.tt(ib[T(tb)], ib[T(tb)], tmp, ALU.mult)
                    cA = s.pcA[d, j]
                    s.ts(tmp, a[T(tb)], cA[3:4], ALU.mult)
                    for n in (2, 1, 0):
                        s.stt(tmp, tmp, cA[n:n + 1], a[T(tb)], ALU.add, ALU.mult)
                    s.ts(a[T(tb)], tmp, 1.0, ALU.add)
            else:
                for tb in range(ntb):
                    s.act(tmpb[T(tb)], a[T(tb)], AF.Exp, scale=s.nsp16[d, j:j + 1])
                    s.act(a[T(tb)], a[T(tb)], AF.Exp, scale=s.nsp8[d, j:j + 1])
                for tb in range(ntb):
                    s.ts(tmpb[T(tb)], tmpb[T(tb)], 1.0, ALU.min)
                    s.tt(ib[T(tb)], ib[T(tb)], xb[T(tb)], ALU.mult)
                for tb in range(ntb):
                    s.act(tmpb[T(tb)], tmpb[T(tb)], AF.Sqrt, scale=-1.0, bias=1.0)
                for tb in range(ntb):
                    s.tt(ib[T(tb)], ib[T(tb)], tmpb[T(tb)], ALU.mult)
            for sg in range(nseg):
                ssl = slice(sg * SL, (sg + 1) * SL)
                init = (s.hinl[d, j:j + 1] if (full and isS) else 0.0)
                if d == 0:
                    s.scan(hf[ssl], a[ssl], ib[ssl], init)
                else:
                    s.scan(hb[ssl][::-1], a[ssl][::-1], ib[ssl][::-1], init)
                if full and not isS:
                    fin = hf[sg * SL + SL - 1:sg * SL + SL] if d == 0 else hb[sg * SL:sg * SL + 1]
                    s.cp(s.finl[j, sg, d:d + 1], fin)
            if not full:
                fin = hf[NT - 1:NT] if d == 0 else hb[0:1]
                s.cp(s.hll[o, d, j:j + 1], fin)

        def gates2(j, wgt):
            A_ = [a, a1]; IB = [ib, ib1]; TM = [hf, hb]
            for tb in range(ntb):
                ps = []
                for d in range(2):
                    pr = s.psv(s.ps_alloc(1), F32, [512])
                    s.mm(pr, wgt[d * 2 + 0], xbb[T(tb)], True, True)
                    pi = s.psv(s.ps_alloc(1), F32, [512])
                    s.mm(pi, wgt[d * 2 + 1], xbb[T(tb)], True, True)
                    ps.append((pr, pi))
                for d in range(2):
                    pr, pi = ps[d]
                    s.act(A_[d][T(tb)], pr, AF.Sigmoid, bias=s.c_br[d, j:j + 1], accum=(None if full else racc2[d, tb:tb + 1]))
                    s.act(IB[d][T(tb)], pi, AF.Sigmoid, bias=s.c_bi[d, j:j + 1])
            if not full:
                for d in range(2):
                    if ntb == 2:
                        s.tt(rsum2[d:d + 1], racc2[d, 0:1], racc2[d, 1:2], ALU.add)
                    else:
                        s.cp(rsum2[d:d + 1], racc2[d, 0:1])
            for d in range(2):
                if not full:
                    s.act(s.pll[o, d, j:j + 1], rsum2[d:d + 1], AF.Exp, scale=s.nsp8[d, j:j + 1])
                for tb in range(ntb):
                    s.act(TM[d][T(tb)], A_[d][T(tb)], AF.Exp, scale=s.nsp16[d, j:j + 1])
                    s.act(A_[d][T(tb)], A_[d][T(tb)], AF.Exp, scale=s.nsp8[d, j:j + 1])
            for d in range(2):
                for tb in range(ntb):
                    s.ts(TM[d][T(tb)], TM[d][T(tb)], 1.0, ALU.min)
                    s.tt(IB[d][T(tb)], IB[d][T(tb)], xb[T(tb)], ALU.mult)
            for d in range(2):
                for tb in range(ntb):
                    s.act(TM[d][T(tb)], TM[d][T(tb)], AF.Sqrt, scale=-1.0, bias=1.0)
            for d in range(2):
                for tb in range(ntb):
                    s.tt(IB[d][T(tb)], IB[d][T(tb)], TM[d][T(tb)], ALU.mult)
            if not full:
                s.scan(hf, a, ib, 0.0)
                s.cp(s.hll[o, 0, j:j + 1], hf[NT - 1:NT])
                s.scan(hb[::-1], a1[::-1], ib1[::-1], 0.0)
                s.cp(s.hll[o, 1, j:j + 1], hb[0:1])
            else:
                for sg in range(nseg):
                    ssl = slice(sg * SL, (sg + 1) * SL)
                    s.scan(hf[ssl], a[ssl], ib[ssl], 0.0)
                    s.cp(s.finl[j, sg, 0:1], hf[sg * SL + SL - 1:sg * SL + SL])
                    s.scan(hb[ssl][::-1], a1[ssl][::-1], ib1[ssl][::-1], 0.0)
                    s.cp(s.finl[j, sg, 1:2], hb[sg * SL:sg * SL + 1])

        banks = proj_u(0)
        yield
        for j in range(16):
            wgt = wq.get()
            zw = wq.get() if full else None
            nxt = proj_u(j + 1) if j + 1 < 16 else None
            conv(j, banks)
            if not both:
                gates(j, wgt, 0)
                gates(j, wgt, 1)
            else:
                gates2(j, wgt)
            wq.done(wgt)
            if full:
                s.tt(hf, hf, hb, ALU.add)
                for tb in range(ntb):
                    pz = s.psv(s.ps_alloc(1), F32, [512])
                    for k in range(NDK):
                        s.mm(pz, zw[k], hT[k, T(tb)], k == 0, k == NDK - 1)
                    s.act(a[T(tb)], pz, AF.Silu)
                    pv = lambda v: v._new(dims=[(NCt, spb), (1, NCt)])
                    s.tt(pv(Yb[j, T(tb)]), nat(hf, tb), pv(a[T(tb)]), ALU.mult)
                wq.done(zw)
            banks = nxt
            yield

    def stage_fold(s):
        bs = Bump(SCR_OFF, SCR_SZ)
        t = s.alloc(bs, F32, [16])
        s.cp(s.hinl, s.c_h0lru)
        for d, order, m0 in ((0, (0, 1, 2), 0), (1, (2, 1, 0), 3)):
            h = s.hinl[d]
            for o in order:
                s.tt(t, s.pll[o, d], h, ALU.mult)
                s.tt(t, t, s.hll[o, d], ALU.add)
                s.tt(t, t, h, ALU.subtract)
                s.stt(h, t, s.c_masks[m0 + o:m0 + o + 1], h, ALU.mult, ALU.add)
        t1 = s.alloc(bs, F32, [2, 128]); t2 = s.alloc(bs, F32, [2, 128])
        s.cp(s.hin5, s.s5h0)
        Pr = s.PT[0].bc(0, 2)
        P2b = s.PT[1:3]
        for pr, order, m0 in (((0, 64), (0, 1, 2), 0), ((64, 64), (2, 1, 0), 3)):
            h = s.hin5.P(*pr)
            for o in order:
                a_, b_ = t1.P(*pr), t2.P(*pr)
                s.tt(a_, h, Pr.P(*pr), ALU.mult)
                s.tt(b_, h[::-1], P2b.P(*pr), ALU.mult)
                s.tt(a_, a_, b_, ALU.add)
                s.tt(a_, a_, s.hl5[o].P(*pr), ALU.add)
                s.tt(a_, a_, h, ALU.subtract)
                s.stt(h, a_, s.c_masks[m0 + o:m0 + o + 1].P(*pr), h, ALU.mult, ALU.add)
        s.tap("hin5", s.hin5)
        s.tap("hinl", s.hinl)

    def stage_merge_out(s, cfg, Ya3, Yb):
        NT, ntb, ci, xd, yd = cfg["NT"], cfg["ntb"], cfg["ci"], cfg["xd"], cfg["yd"]
        isS = cfg["name"] == "S"
        hTb = s.sb(R1_OFF, BF16, [NDK, 512])
        mg = s.sb(R1_OFF + 32768, BF16, [NDK, 512])
        bs = Bump(SCR_OFF, SCR_SZ)
        sa = s.alloc(bs, F32, [512]); sb_ = s.alloc(bs, F32, [512]); t1 = s.alloc(bs, F32, [512]); t2 = s.alloc(bs, F32, [512])
        xt = [s.alloc(bs, F32, [512]) for _ in range(2)]
        xn = [s.alloc(bs, F32, [512]) for _ in range(2)]
        sq = s.alloc(bs, F32, [512]); rs = s.alloc(bs, F32, [512])
        for tb in range(ntb):
            tsl = slice(tb * 512, (tb + 1) * 512)
            if isS and tb == 0:
                s.dma(hTb, s.d_hts[:, tsl])
            items = []
            for m in range(NDK):
                items += [(s.dsel(s.d_wos, None, m), [16, 128]), (s.dsel(s.d_win, None, 64 + m), [NDK, 128]),
                          (s.dsel(s.d_wol, None, m), [16, 128]), (s.dsel(s.d_win, None, 96 + m), [NDK, 128])]
            for dt in range(NDK):
                items.append((s.dsel(s.d_wo, None, dt), [NDK, 128]))
            wq = WQ(s, items)
            for m in range(NDK):
                wos = wq.get()
                pya = s.psv(s.ps_alloc(1), F32, [512])
                for k in range(16):
                    s.mm(pya, wos[k], Ya3[k, tsl], k == 0, k == 15)
                wq.done()
                wga = wq.get()
                pga = s.psv(s.ps_alloc(1), F32, [512])
                for k in range(NDK):
                    s.mm(pga, wga[k], hTb[k], k == 0, k == NDK - 1)
                wq.done()
                wol = wq.get()
                pyb = s.psv(s.ps_alloc(1), F32, [512])
                for k in range(16):
                    s.mm(pyb, wol[k], Yb[k, tsl], k == 0, k == 15)
                wq.done()
                wgb = wq.get()
                pgb = s.psv(s.ps_alloc(1), F32, [512])
                for k in range(NDK):
                    s.mm(pgb, wgb[k], hTb[k], k == 0, k == NDK - 1)
                wq.done()
                s.act(sa, pga, AF.Sigmoid)
                s.act(sb_, pgb, AF.Sigmoid)
                s.tt(t1, sa, pya, ALU.mult)
                s.tt(t2, sb_, pyb, ALU.mult)
                s.tt(mg[m], t1, t2, ALU.add)
            if tb == 0:
                s.tap("mg_" + cfg["name"], mg, BF16)
            if isS and tb + 1 < ntb:
                s.dma(hTb, s.d_hts[:, (tb + 1) * 512:(tb + 2) * 512])
            last = (tb == ntb - 1)
            xres = s.sb(R2_OFF, F32, [NDK, 512]) if last else None
            pss = s.ps_alloc(1, hold=True)
            pssv = s.psv(pss, F32, [512])
            for dt in range(NDK):
                wo = wq.get()
                po = s.psv(s.ps_alloc(1), F32, [512])
                for k in range(NDK):
                    s.mm(po, wo[k], mg[k], k == 0, k == NDK - 1)
                wq.done()
                x_, n_ = xt[dt % 2], (xres[dt] if last else xn[dt % 2])
                s.dma(x_, xd[dt, tsl])
                s.stt(n_, po, s.gt[dt, ci:ci + 1], x_, ALU.mult, ALU.add)
                s.act(sq, n_, AF.Square)
                s.mm(pssv, s.ones32, sq, dt == 0, dt == NDK - 1, signal=True)
                if not last:
                    s.dma(yd[dt, tsl].key((dt, tb)), n_)
            s.ts(rs, pssv, 1.0 / D, ALU.mult, EPS, ALU.add)
            s.ps_release(pss)
            s.act(rs, rs, AF.Sqrt)
            s.recip(rs, rs)
            if last:
                for dt in range(NDK):
                    n_ = xn[dt % 2]
                    s.stt(n_, xres[dt], s.c_fg[dt:dt + 1], rs, ALU.mult, ALU.mult)
                    s.dma(yd[dt, tsl].key((dt, tb)), n_)
                continue
            s.dma(xt[0], yd[0, tsl].key((0, tb)))
            for dt in range(NDK):
                x_, n_ = xt[dt % 2], xn[dt % 2]
                if dt + 1 < NDK:
                    s.dma(xt[(dt + 1) % 2], yd[dt + 1, tsl].key((dt + 1, tb)))
                s.stt(n_, x_, s.c_fg[dt:dt + 1], rs, ALU.mult, ALU.mult)
                s.dma(yd[dt, tsl].key((dt, tb)), n_)

    def stage_consts(s):
        cb = s.cb
        A = lambda dt, sh: s.alloc(cb, dt, sh)
        s.vecs = A(F32, [392])
        s.dma(s.vecs, s.d_vecs)
        o = 0

        def take(n, shape=None):
            nonlocal o
            v = s.vecs[o:o + n]
            o += n
            if shape:
                dims = []
                stt_ = 1
                for c in reversed(shape):
                    dims.insert(0, (stt_, c))
                    stt_ *= c
                v = v._new(dims=dims)
            return v

        s.c_bada = take(96)
        s.c_ng = take(32)
        s.c_fg = take(32)
        s.c_bglu = take(16)
        s.c_convw = take(64, [4, 16])
        s.c_convb = take(16)
        s.c_br = take(32, [2, 16])
        s.c_bi = take(32, [2, 16])
        s.c_lam = take(32, [2, 16])
        s.c_h0lru = take(32, [2, 16])
        s.c_masks = take(6)
        s.identf = A(F32, [128])
        s.dma(s.identf, s.d_consts[0])
        s.identb = A(BF16, [128])
        s.cp(s.identb, s.identf)
        s.ones32 = A(F32, [128])
        s.memset(s.ones32, 1.0)
        s.condT = A(F32, [NDK, 2])
        s.dma(s.condT, s.d_cond)
        s.sc = A(BF16, [NDK, 2])
        s.act(s.sc, s.condT, AF.Silu)
        s.modv = A(F32, [96, 2])
        s.gs = A(F32, [NDK, 2])
        s.sh = s.modv[0:32]
        s.gt = s.modv[64:96]
        s.nsp8 = A(F32, [2, 16])
        s.nsp16 = A(F32, [2, 16])
        ep = A(F32, [2, 16]); t = A(F32, [2, 16]); t2_ = A(F32, [2, 16]); msk = A(F32, [2, 16])
        s.act(ep, s.c_lam, AF.Exp, scale=-1.0)
        s.act(t, ep, AF.Ln, scale=1.0, bias=1.0)
        s.ts(t2_, ep, 1.0 / 3.0, ALU.mult, -0.5, ALU.add)
        s.tt(t2_, t2_, ep, ALU.mult)
        s.ts(t2_, t2_, 1.0, ALU.add)
        s.tt(t2_, t2_, ep, ALU.mult)
        s.ts(msk, ep, 0.03, ALU.is_lt)
        s.tt(t2_, t2_, t, ALU.subtract)
        s.tt(t2_, t2_, msk, ALU.mult)
        s.tt(t, t, t2_, ALU.add)
        s.ts(s.nsp8, t, -8.0, ALU.mult)
        s.ts(s.nsp16, t, -16.0, ALU.mult)
        s.pcO = A(F32, [2, 16, 5])
        s.pcA = A(F32, [2, 16, 4])
        pw = A(F32, [2, 16])
        s.cp(pw, s.nsp16)
        f = 1.0
        for n in range(1, 6):
            f *= n
            s.ts(s.pcO[:, :, n - 1], pw, -1.0 / f, ALU.mult)
            if n < 5:
                s.tt(pw, pw, s.nsp16, ALU.mult)
        s.cp(pw, s.nsp8)
        f = 1.0
        for n in range(1, 5):
            f *= n
            s.ts(s.pcA[:, :, n - 1], pw, 1.0 / f, ALU.mult)
            if n < 4:
                s.tt(pw, pw, s.nsp8, ALU.mult)
        s.s5h0 = A(F32, [2, 128])
        s.dma(s.s5h0, s.d_s5h0)
        s.hin5 = A(F32, [2, 128])
        s.hl5 = A(F32, [3, 2, 128])
        s.PT = A(F32, [3, 128])
        s.hinl = A(F32, [2, 16])
        s.hll = A(F32, [3, 2, 16])
        s.pll = A(F32, [3, 2, 16])
        s.finl = A(F32, [16, 2, 2])
        s.fin5 = A(F32, [2, 2, 128])
        s.zero1 = A(F32, [1])
        s.memset(s.zero1, 0.0)

    def stage_mod(s):
        pm_bank = s.ps_alloc(1, hold=True)
        pm = s.psv(pm_bank, F32, [96, 2])
        for i in range(24):
            slot = s.sb(R1_OFF + (i % 2) * 32768, BF16, [NDK, 512])
            s.dma(slot, s.dsel(s.d_wada, None, i), q="gpsimd")
            for ct in range(4):
                for dk in range(NDK):
                    first = (i == 0 and ct == 0 and dk == 0)
                    last = (i == 23 and ct == 3 and dk == NDK - 1)
                    s.mm(pm[i * 4 + ct], slot[dk, ct * 128:(ct + 1) * 128], s.sc[dk], start=first, stop=last,
                         signal=(dk == NDK - 1 and ct == 3))
            yield
        s.tt(s.modv, pm, s.c_bada.bc(1, 2), ALU.add)
        s.ps_release(pm_bank)
        s.ts(s.gs, s.modv[32:64], 1.0, ALU.add)
        s.tt(s.gs, s.gs, s.c_ng.bc(1, 2), ALU.mult)
        s.tap("modv", s.modv)
        s.tap("gs", s.gs)

    def cmul(s, out2, x2, Er, Ei, t1, t2, pr=None):
        xs = x2[::-1]
        Erb = Er.bc(0, 2)
        Eib = Ei.bc(0, 2)
        vs = [out2, x2, xs, t1, t2, Erb, Eib]
        if pr is not None:
            vs = [v.P(*pr) for v in vs]
        out2, x2, xs, t1, t2, Erb, Eib = vs
        s.tt(t1, x2, Erb, ALU.mult)
        s.tt(t2, xs, Eib, ALU.mult)
        s.tt(out2[0], t1[0], t2[0], ALU.subtract)
        s.tt(out2[1], t1[1], t2[1], ALU.add)

    def stage_tables(s):
        b = Bump(R2_OFF, SB_TOTAL - R2_OFF)
        A = lambda dt, sh: s.alloc(b, dt, sh)
        small = A(F32, [3, 128])
        s.dma(small, s.d_s5small)
        masks = A(F32, [2, 128])
        s.dma(masks, s.d_consts[1:3])
        s.maskF, s.maskB = masks[0], masks[1]
        lamr, lami, lstep = small[0], small[1], small[2]
        dt_ = A(F32, [128]); ar = A(F32, [128]); th = A(F32, [128]); em1 = A(F32, [128]); rho = A(F32, [128])
        kf = A(F32, [128]); ki = s.alloc(b, I32, [128]); s2 = A(F32, [128]); c2 = A(F32, [128])
        sn = A(F32, [128]); nr = A(F32, [128]); tq = A(F32, [128]); tq2 = A(F32, [128])
        L2 = A(F32, [2, 128])
        beta2 = A(F32, [2, 128])
        betar, betai = beta2[0], beta2[1]
        s.ts(ki, lstep, 1.0 / float(np.log(2.0)), ALU.mult)
        s.cp(kf, ki)
        s.stt(tq, kf, -0.693359375, lstep, ALU.mult, ALU.add)
        s.stt(tq, kf, 2.12194440e-4, tq, ALU.mult, ALU.add)
        s.ts(dt_, tq, 1.0 / 9.0, ALU.mult, 1.0, ALU.add)
        for cst in (1.0 / 8.0, 1.0 / 7.0, 1.0 / 6.0, 1.0 / 5.0, 1.0 / 4.0, 1.0 / 3.0, 1.0 / 2.0, 1.0):
            s.tt(dt_, dt_, tq, ALU.mult)
            s.ts(dt_, dt_, cst, ALU.mult, 1.0, ALU.add)
        s.ts(ki, ki, 127.0, ALU.add)
        s.ts(ki, ki, 23, ALU.logical_shift_left)
        kpow = s.sb(ki.off * 4, F32, [128])
        s.tt(dt_, dt_, kpow, ALU.mult)
        s.tap("dt", dt_)
        s.tt(ar, lamr, dt_, ALU.mult)
        s.tt(th, lami, dt_, ALU.mult)
        s.ts(em1, ar, 1.0 / 6.0, ALU.mult, 1.0, ALU.add)
        for cst in (1.0 / 5.0, 1.0 / 4.0, 1.0 / 3.0, 1.0 / 2.0):
            s.tt(em1, em1, ar, ALU.mult)
            s.ts(em1, em1, cst, ALU.mult, 1.0, ALU.add)
        s.tt(em1, em1, ar, ALU.mult)
        s.ts(rho, em1, 1.0, ALU.add)
        s.ts(ki, th, 1.0 / (2.0 * np.pi), ALU.mult)
        s.cp(kf, ki)
        s.stt(th, kf, -2.0 * np.pi, th, ALU.mult, ALU.add)
        s.act(s2, th, AF.Sin, scale=0.5)
        s.ts(tq, th, -1.0, ALU.mult)
        s.tt(tq, tq, th, ALU.max)
        s.ts(tq, tq, -0.5, ALU.mult, float(np.pi / 2), ALU.add)
        s.act(c2, tq, AF.Sin)
        s.tt(sn, s2, c2, ALU.mult)
        s.ts(sn, sn, 2.0, ALU.mult)
        s.tt(tq, s2, s2, ALU.mult)
        s.ts(tq, tq, -2.0, ALU.mult)
        s.tt(nr, rho, tq, ALU.mult)
        s.tt(nr, nr, em1, ALU.add)
        s.ts(L2[0], nr, 1.0, ALU.add)
        s.tt(L2[1], rho, sn, ALU.mult)
        s.tt(tq, lamr, lamr, ALU.mult)
        s.tt(tq2, lami, lami, ALU.mult)
        s.tt(tq, tq, tq2, ALU.add)
        s.recip(tq, tq)
        s.tt(betar, nr, lamr, ALU.mult)
        s.tt(tq2, L2[1], lami, ALU.mult)
        s.tt(betar, betar, tq2, ALU.add)
        s.tt(betar, betar, tq, ALU.mult)
        s.tt(betai, L2[1], lamr, ALU.mult)
        s.tt(tq2, nr, lami, ALU.mult)
        s.tt(betai, betai, tq2, ALU.subtract)
        s.tt(betai, betai, tq, ALU.mult)
        E2 = A(F32, [9, 2, 128])
        N2 = A(F32, [8, 2, 128])
        AP2 = A(F32, [8, 2, 128])
        t1s = A(F32, [2, 128]); t2s = A(F32, [2, 128])

        def cmul_small(out2, x2, Y2):
            s.cmul(out2, x2, Y2[0], Y2[1], t1s, t2s)

        s.memset(E2[0, 0], 1.0)
        s.memset(E2[0, 1], 0.0)
        s.cp(E2[1], L2)
        for e in range(2, 9):
            cmul_small(E2[e], E2[e - 1], L2)
        N1 = N2[1]
        s.tt(tq, L2[0], L2[0], ALU.mult)
        s.tt(tq2, L2[1], L2[1], ALU.mult)
        s.tt(tq, tq, tq2, ALU.add)
        s.recip(tq, tq)
        s.tt(N1[0], L2[0], tq, ALU.mult)
        s.tt(N1[1], L2[1], tq, ALU.mult)
        s.ts(N1[1], N1[1], -1.0, ALU.mult)
        s.memset(N2[0, 0], 1.0)
        s.memset(N2[0, 1], 0.0)
        for e in range(2, 8):
            cmul_small(N2[e], N2[e - 1], N2[1])
        s.cp(AP2[0], E2[8])
        for k in range(1, 8):
            cmul_small(AP2[k], AP2[k - 1], AP2[k - 1])
        s.cp(s.PT[0], AP2[7, 0])
        s.ts(s.PT[1], AP2[7, 1], -1.0, ALU.mult)
        s.cp(s.PT[2], AP2[7, 1])
        s.tap("E2", E2)
        s.tap("N2", N2)
        s.tap("AP2", AP2)
        s.tap("beta2", beta2)
        Dm = A(F32, [128])
        s.dma(Dm, s.d_s5Dm)
        def cmulE(out4, x2, Et, Etb, ne, gs_, T1, T2):
            Er = Et[0:ne, 0, gs_].T(1, 0).bc(2, 16)
            Ei = Et[0:ne, 1, gs_].T(1, 0).bc(2, 16)
            xr, xi = x2[0].bc(1, ne), x2[1].bc(1, ne)
            t1, t2 = T1[0, :, 0:ne, :], T2[0, :, 0:ne, :]
            s.tt(t1, xr, Er, ALU.mult)
            s.tt(t2, xi, Ei, ALU.mult)
            s.tt(out4[0], t1, t2, ALU.subtract)
            s.tt(t1, xr, Ei, ALU.mult)
            s.tt(t2, xi, Er, ALU.mult)
            s.tt(out4[1], t1, t2, ALU.add)

        E2b = N2b = None
        GB = 8
        BC = [(A(F32, [2, GB, 16]), A(F32, [2, GB, 16])) for _ in range(2)]
        s.dma(BC[0][0], s.d_s5B[:, 0:GB, :])
        s.dma(BC[0][1], s.d_s5C[:, 0:GB, :])
        mark = b.cur
        yield
        for i in range(128 // GB):
            b.cur = mark
            g0 = i * GB
            gs_ = slice(g0, g0 + GB)
            B2, C2 = BC[i % 2]
            if i + 1 < 128 // GB:
                gn = slice(g0 + GB, g0 + 2 * GB)
                s.dma(BC[(i + 1) % 2][0], s.d_s5B[:, gn, :])
                s.dma(BC[(i + 1) % 2][1], s.d_s5C[:, gn, :])
            G2 = A(F32, [2, GB, 16]); XC2 = A(F32, [2, GB, 16])
            T1 = A(F32, [2, GB, 9, 16]); T2 = A(F32, [2, GB, 9, 16])
            bcK = lambda v: v.bc(1, 16)
            s.cmul(G2, B2, bcK(betar[gs_]), bcK(betai[gs_]), T1[:, :, 0, :], T2[:, :, 0, :])
            s.cp(XC2.P(0, 64), G2.P(0, 64), eng="scalar")
            s.cp(XC2.P(64, 64), C2.P(64, 64), eng="scalar")
            Rw = A(F32, [2, GB, 8, 16])
            RQ = A(F32, [2, GB, 8, 16])
            Qp = A(F32, [2, GB, 9, 16])
            cmulE(Rw, G2, E2, E2b, 8, gs_, T1, T2)
            cmulE(RQ, XC2, N2, N2b, 8, gs_, T1, T2)
            cmulE(Qp, C2, E2, E2b, 9, gs_, T1, T2)
            Rn, Qn = RQ, RQ
            s.ts(Qp[1], Qp[1], -1.0, ALU.mult)
            s.ts(Qn[1].P(64, 64), Qn[1].P(64, 64), -1.0, ALU.mult)
            RwS = T2._new(dims=[(GB * 9 * 16, 2), (128, GB), (16, 8), (1, 16)])
            for comp in range(2):
                s.cp(RwS[comp].P(0, 64), Rw[comp, :, ::-1, :].P(0, 64))
                s.cp(RwS[comp].P(64, 64), Rw[comp].P(64, 64), eng="scalar")
            m1 = T1._new(dims=[(128, 4), (1, 128)])
            m2 = T1._new(off=T1.off + 512, dims=[(128, 4), (1, 128)])
            WT = [A(BF16, [8, 128]), A(BF16, [8, 128])]
            Mt = A(BF16, [8, 128]); Ot = [A(BF16, [8, 128]), A(BF16, [8, 128])]
            AT = A(F32, [8, 3, 8])
            for comp in range(2):
                pb = s.ps_alloc(2)
                pt = s.psv(pb, F32, [8, 128])
                for g in range(8):
                    s.trp(pt[g], RwS[comp, g].merge(), s.identf)
                s.cp(WT[comp][0:4], pt[0:4], eng="scalar")
                s.cp(WT[comp][4:8], pt[4:8], eng="scalar")
            for comp in range(2):
                ov = Ot[comp]._new(dims=[(128, 8), (16, 8), (1, 16)])
                s.cp(ov.P(0, 64), Qp[comp, :, 1:9, :].P(0, 64), eng="scalar")
                s.cp(ov.P(64, 64), Qp[comp, :, 8:0:-1, :].P(64, 64))
            for q4 in range(2):
                pf = s.ps_alloc(1)
                pbk = s.ps_alloc(1)
                Pf = s.psv(pf, F32, [4, 128]); Pb = s.psv(pbk, F32, [4, 128])
                for gg in range(4):
                    g = q4 * 4 + gg
                    s.mm(Pf[gg], Rn[0, g].merge().P(0, 64), Qp[0, g, 0:8, :].merge().P(0, 64), True, False)
                    s.mm(Pf[gg], Rn[1, g].merge().P(0, 64), Qp[1, g, 0:8, :].merge().P(0, 64), False, True)
                    s.mm(Pb[gg], Rw[0, g].merge().P(64, 64), Qn[0, g].merge().P(64, 64), True, False)
                    s.mm(Pb[gg], Rw[1, g].merge().P(64, 64), Qn[1, g].merge().P(64, 64), False, True)
                s.tt(m1, Pf, s.maskF.bc(0, 4), ALU.mult)
                s.tt(m2, Pb, s.maskB.bc(0, 4), ALU.mult)
                s.tt(m1, m1, m2, ALU.add)
                gsl = slice(g0 + q4 * 4, g0 + q4 * 4 + 4)
                s.tt(m2, s.identf.bc(0, 4), Dm[gsl].bc(1, 128), ALU.mult)
                s.tt(Mt[q4 * 4:q4 * 4 + 4], m1, m2, ALU.add)
            s.cp(AT[:, 0, :], AP2[:, 0, gs_])
            s.ts(AT[:, 1, :], AP2[:, 1, gs_], -1.0, ALU.mult)
            s.cp(AT[:, 2, :], AP2[:, 1, gs_])
            for nm, tv in (("WTr", WT[0]), ("WTi", WT[1]), ("M", Mt), ("Or", Ot[0]), ("Oi", Ot[1])):
                s.dma(s.dsel(s.d_tab[nm], None, i, dkey=i), tv.merge())
            s.dma(s.dsel(s.d_tabAT, None, i, dkey=i), AT.merge())
            yield
        if "tabs" in s.dbg:
            for nm in ("WTr", "WTi", "M", "Or", "Oi"):
                d = s.dram("dbg_tab_" + nm, [16, 128, 1024], BF16, kind="ExternalOutput")
                s.taps.append(("dbg_tab_" + nm, [16, 128, 1024]))
                for i in range(16):
                    s.dma(s.dsel(d, None, i, dkey=i), s.dsel(s.d_tab[nm], None, i, dkey=i))
            d = s.dram("dbg_tab_AT", [16, 128, 192], F32, kind="ExternalOutput")
            s.taps.append(("dbg_tab_AT", [16, 128, 192]))
            for i in range(16):
                s.dma(s.dsel(d, None, i, dkey=i), s.dsel(s.d_tabAT, None, i, dkey=i))


def _fm(v):
    v = np.asarray(v, np.float32).reshape(-1, 128)
    return np.ascontiguousarray(v.T)


def _slabs(W, ncols_per=128):
    Kd, N = W.shape
    a = W.reshape(Kd // 128, 128, N // ncols_per, ncols_per)
    return np.ascontiguousarray(a.transpose(2, 1, 0, 3))


def _xT(x):
    T = x.shape[0]
    a = x.reshape(T // 8, 8, NDK, 128).transpose(3, 2, 1, 0)
    return np.ascontiguousarray(a.reshape(128, NDK, T))


def _unT(yT):
    T = yT.shape[2]
    a = np.asarray(yT).reshape(128, NDK, 8, T // 8).transpose(3, 2, 1, 0)
    return np.ascontiguousarray(a.reshape(T, D))


def prep_shared(inp):
    sh = {}
    sh["w_ada_s"] = _slabs(inp["w_ada"][0], 512)
    sh["w_in_s"] = _slabs(inp["w_in"][0], 128)
    sh["w_glu_s"] = _slabs(inp["s5_w_glu"][0], 128)
    sh["w_os_s"] = _slabs(inp["w_out_s5"][0], 128)
    sh["w_ol_s"] = _slabs(inp["w_out_lru"][0], 128)
    sh["w_o_s"] = _slabs(inp["w_out"][0], 128)
    wr, wi = inp["lru_w_r"][0], inp["lru_w_i"][0]
    wg = np.stack([wr[0], wi[0], wr[1], wi[1]], axis=0)
    sh["w_gate"] = np.ascontiguousarray(wg.transpose(1, 2, 0, 3))
    lr, li, ls = inp["s5_lam_re"][0], inp["s5_lam_im"][0], inp["s5_log_step"][0]
    pk = lambda a: np.ascontiguousarray(a.transpose(0, 2, 1).reshape(128, 128))
    lsb = np.broadcast_to(ls[:, None, :], (2, 64, 128)).reshape(128, 128)
    sh["s5small"] = np.ascontiguousarray(np.stack([pk(lr), pk(li), lsb], axis=1).astype(np.float32))
    br, bi = inp["s5_b_re"][0], inp["s5_b_im"][0]
    pb = lambda a: a.transpose(0, 2, 1, 3).reshape(128, 128, 16)
    sh["s5B"] = np.ascontiguousarray(np.stack([pb(br), pb(bi)], axis=1))
    cr, ci = inp["s5_c_re"][0], inp["s5_c_im"][0]
    pc = lambda a: a.transpose(0, 3, 1, 2).reshape(128, 128, 16)
    sh["s5C"] = np.ascontiguousarray(np.stack([pc(cr), pc(ci)], axis=1))
    d = inp["s5_d"][0].reshape(128, 16)
    sh["s5Dm"] = np.ascontiguousarray(np.broadcast_to(d.T[None, :, :], (8, 16, 128)).reshape(128, 128))
    ident = np.eye(128, dtype=np.float32)
    sidx = np.arange(128) // 16
    maskF = (sidx[None, :] >= sidx[:, None]).astype(np.float32)
    maskB = (sidx[None, :] <= sidx[:, None]).astype(np.float32)
    sh["consts"] = np.ascontiguousarray(np.stack([ident, maskF, maskB], axis=1))
    return sh


def prep_core(inp, r):
    b, q = r // 4, r % 4
    m = {}
    xs = inp["x_sample"][b]
    m["xT_own"] = _xT(xs[q * 1024:(q + 1) * 1024])
    others = [j for j in range(4) if j != q]
    m["xT_oth"] = np.stack([_xT(xs[j * 1024:(j + 1) * 1024]) for j in others], axis=0)
    m["xT_p"] = _xT(inp["x_prompt"][2 * r:2 * r + 2].reshape(512, D))
    m["condT"] = np.ascontiguousarray(np.stack([_fm(inp["c"][b]), _fm(inp["c_ctx"])], axis=2))
    mf = np.array([1.0 if j < q else 0.0 for j in others], np.float32)
    mb = np.array([1.0 if j > q else 0.0 for j in others], np.float32)
    fm2 = lambda a: np.concatenate([_fm(a[0]), _fm(a[1])], axis=1)
    vec = np.concatenate([
        _fm(inp["b_ada"][0]), _fm(inp["norm_g"][0]), _fm(inp["final_g"]), _fm(inp["s5_b_glu"][0]),
        np.concatenate([_fm(inp["lru_conv_w"][0][k]) for k in range(4)], axis=1),
        _fm(inp["lru_conv_b"][0]), fm2(inp["lru_b_r"][0]), fm2(inp["lru_b_i"][0]), fm2(inp["lru_lam"][0]),
        fm2(inp["state_lru"][b, 0]),
        np.broadcast_to(np.concatenate([mf, mb])[None, :], (128, 6)),
    ], axis=1).astype(np.float32)
    assert vec.shape == (128, 390), vec.shape
    m["vecs"] = np.ascontiguousarray(np.pad(vec, ((0, 0), (0, 2))))
    sr, si = inp["state_s5_re"][b, 0], inp["state_s5_im"][b, 0]
    pk = lambda a: a.transpose(0, 2, 1).reshape(128, 128)
    m["s5h0"] = np.ascontiguousarray(np.stack([pk(sr), pk(si)], axis=1))
    return m


_CACHE = {}


def kernel(**inputs):
    inp = {k: np.asarray(v) for k, v in inputs.items()}
    if "nc" not in _CACHE:
        _CACHE["nc"] = K().build()
    nc = _CACHE["nc"]
    sh = prep_shared(inp)
    in_maps = []
    for r in range(NCORES):
        m = dict(sh)
        m.update(prep_core(inp, r))
        in_maps.append(m)
    res = run_bass_kernel_spmd(nc, in_maps, core_ids=list(range(NCORES)))
    return assemble(res.results)


def assemble(results):
    y_prompt = np.zeros((16, 256, D), np.float32)
    y_sample = np.zeros((2, 4096, D), np.float32)
    s_re = np.zeros((16, 1, 2, 128, 64), np.float32)
    s_im = np.zeros((16, 1, 2, 128, 64), np.float32)
    s_lru = np.zeros((16, 1, 2, 2048), np.float32)
    for r in range(NCORES):
        b, q = r // 4, r % 4
        o = results[r]
        ys = _unT(o["yT_s"])
        y_sample[b, q * 1024:(q + 1) * 1024] = ys
        yp = _unT(o["yT_p"]).reshape(2, 256, D)
        y_prompt[2 * r:2 * r + 2] = yp
        sl = np.asarray(o["st_lru"])
        s_lru[2 * r:2 * r + 2, 0] = sl.transpose(2, 3, 1, 0).reshape(2, 2, 2048)
        s5 = np.asarray(o["st_s5"]).reshape(2, 64, 2, 2, 128)
        s_re[2 * r:2 * r + 2, 0] = s5[:, :, :, 0, :].transpose(2, 0, 3, 1)
        s_im[2 * r:2 * r + 2, 0] = s5[:, :, :, 1, :].transpose(2, 0, 3, 1)
    return (y_prompt, y_sample, s_re, s_im, s_lru)
```

```python
import numpy as np
from contextlib import ExitStack
import concourse.bass as bass
import concourse.mybir as mybir
from concourse.bass_utils import run_bass_kernel_spmd

F32 = mybir.dt.float32
BF16 = mybir.dt.bfloat16
I32 = mybir.dt.int32
AF = mybir.ActivationFunctionType
ALU = mybir.AluOpType
ITEM = {F32: 4, BF16: 2, I32: 4}
ENGS = ["tensor", "vector", "scalar", "gpsimd", "sync"]

D = 4096
NDK = 32
EPS = 1e-6
NCORES = 8
LRU_POLY = False

CONST_OFF, CONST_SZ = 0, 16384
R1_OFF, R1_SZ = 16384, 65536
R2_OFF, R2_SZ = 81920, 32768
R3_OFF, R3_SZ = 114688, 32768
WP_OFF, WP_SZ = 147456, 32768
SCR_OFF, SCR_SZ = 180224, 24576
SB_TOTAL = 204800
PS_TOTAL = 16384
CELL = 64


class V:
    def __init__(s, h, space, pstep, item, off, dims, p0=0, pn=128, dkey=None):
        s.h, s.space, s.pstep, s.item, s.off, s.dims, s.p0, s.pn, s.dkey = h, space, pstep, item, off, list(dims), p0, pn, dkey

    def _new(s, off=None, dims=None, p0=None, pn=None, dkey=None):
        return V(s.h, s.space, s.pstep, s.item, s.off if off is None else off, s.dims if dims is None else dims,
                 s.p0 if p0 is None else p0, s.pn if pn is None else pn, s.dkey if dkey is None else dkey)

    def ap(s):
        return bass.AP(s.h, s.p0 * s.pstep + s.off, [[s.pstep, s.pn]] + [[a, b] for a, b in s.dims])

    def __getitem__(s, idx):
        if not isinstance(idx, tuple):
            idx = (idx,)
        off = s.off
        dims = []
        for i, (st, c) in enumerate(s.dims):
            if i < len(idx):
                ix = idx[i]
                if isinstance(ix, int):
                    assert 0 <= ix < c, (ix, c)
                    off += st * ix
                else:
                    a, b, step = ix.indices(c)
                    n = len(range(a, b, step))
                    assert n > 0
                    off += st * a
                    dims.append((st * step, n))
            else:
                dims.append((st, c))
        return s._new(off=off, dims=dims)

    def P(s, p0, pn):
        return s._new(p0=s.p0 + p0, pn=pn)

    def T(s, *perm):
        return s._new(dims=[s.dims[i] for i in perm])

    def bc(s, axis, n):
        d = list(s.dims)
        d.insert(axis, (0, n))
        return s._new(dims=d)

    def merge(s):
        d = s.dims
        tot = 1
        for a, b in d:
            tot *= b
        st = d[-1][0]
        exp = st
        for a, b in reversed(d):
            assert a == exp, ("not mergeable", d)
            exp *= b
        return s._new(dims=[(st, tot)])

    def key(s, k):
        return s._new(dkey=k)

    @property
    def shape(s):
        return [c for _, c in s.dims]

    def span(s):
        lo = hi = s.off
        for st, c in s.dims:
            if st >= 0:
                hi += st * (c - 1)
            else:
                lo += st * (c - 1)
        return lo * s.item, (hi + 1) * s.item


class Track:
    def __init__(s, ncell):
        s.n = ncell
        s.lastw = np.zeros(ncell, np.int64)
        s.reads = {}

    def deps(s, c0, c1, write):
        out = set(np.unique(s.lastw[c0:c1]).tolist())
        if write:
            for sid, arr in s.reads.items():
                m = int(arr[c0:c1].max())
                if m:
                    out.add((sid << 32) | m)
        out.discard(0)
        return out

    def rec(s, c0, c1, write, sid, val):
        if write:
            s.lastw[c0:c1] = (sid << 32) | val
            for arr in s.reads.values():
                arr[c0:c1] = 0
        else:
            arr = s.reads.get(sid)
            if arr is None:
                arr = s.reads[sid] = np.zeros(s.n, np.int64)
            np.maximum(arr[c0:c1], val, out=arr[c0:c1])


class Prog:
    def __init__(s, nc, es):
        s.nc, s.es = nc, es
        s.q = {e: [] for e in ENGS}
        s.cnt = {e: 0 for e in ENGS}
        s.sems = []
        s.esem = {e: s._newsem("e_" + e) for e in ENGS}
        s.waited = {e: {} for e in ENGS}
        s.tr = {"sb": Track(SB_TOTAL // CELL), "ps": Track(PS_TOTAL // CELL)}
        s.dtr = {}
        s.lanes = {"sync": [], "gpsimd": []}
        s.lane_rr = {"sync": 0, "gpsimd": 0}
        s.nlanes = {"sync": 24, "gpsimd": 8}
        s.lane_cnt = {}
        s.ninstr = 0

    def _newsem(s, name):
        h = s.es.enter_context(s.nc.semaphore(name))
        s.sems.append(h)
        return len(s.sems) - 1

    def _cells(s, v):
        if v.space in ("sb", "ps"):
            lo, hi = v.span()
            return s.tr[v.space], lo // CELL, (hi + CELL - 1) // CELL
        k = (v.h.name, v.dkey)
        t = s.dtr.get(k)
        if t is None:
            t = s.dtr[k] = Track(1)
        return t, 0, 1

    def _deps(s, ins, outs):
        d = set()
        for v in ins:
            t, a, b = s._cells(v)
            d |= t.deps(a, b, False)
        for v in outs:
            t, a, b = s._cells(v)
            d |= t.deps(a, b, True)
        return d

    def _emit_waits(s, eng, deps):
        best = {}
        for x in deps:
            sid, val = x >> 32, x & 0xFFFFFFFF
            if eng == "tensor" and sid == s.esem["tensor"]:
                continue
            if val > best.get(sid, 0):
                best[sid] = val
        for sid, val in best.items():
            if s.waited[eng].get(sid, 0) >= val:
                continue
            s.waited[eng][sid] = val
            sem = s.sems[sid]
            s.q[eng].append(lambda E, sem=sem, val=val: E.wait_ge(sem, val))
            s.ninstr += 1

    def _rec(s, ins, outs, sid, val):
        for v in ins:
            t, a, b = s._cells(v)
            t.rec(a, b, False, sid, val)
        for v in outs:
            t, a, b = s._cells(v)
            t.rec(a, b, True, sid, val)

    def op(s, eng, fn, ins=(), outs=(), signal=True):
        ins = [v for v in ins if isinstance(v, V)]
        s._emit_waits(eng, s._deps(ins, outs))
        sid = s.esem[eng]
        seq = s.cnt[eng] + 1
        sem = s.sems[sid]
        if signal:
            s.cnt[eng] = seq
            s.q[eng].append(lambda E, fn=fn, sem=sem: fn(E).then_inc(sem, 1))
        else:
            s.q[eng].append(lambda E, fn=fn: fn(E))
        s.ninstr += 1
        s._rec(ins, outs, sid, seq)

    def dma(s, q, out, in_):
        lanes = s.lanes[q]
        idx = s.lane_rr[q] % s.nlanes[q]
        if idx >= len(lanes):
            lanes.append(s._newsem("d_%s%d" % (q, len(lanes))))
        sid = lanes[idx]
        s.lane_rr[q] += 1
        prev = s.lane_cnt.get(sid, 0)
        deps = s._deps([in_], [out])
        if prev:
            deps.add((sid << 32) | prev)
        s._emit_waits(q, deps)
        val = prev + 16
        s.lane_cnt[sid] = val
        sem = s.sems[sid]
        oa, ia = out.ap(), in_.ap()
        s.q[q].append(lambda E, oa=oa, ia=ia, sem=sem: E.dma_start(out=oa, in_=ia).then_inc(sem, 16))
        s.ninstr += 1
        s._rec([in_], [out], sid, val)

    def finish(s):
        for sid, val in s.lane_cnt.items():
            sem = s.sems[sid]
            s.q["sync"].append(lambda E, sem=sem, val=val: E.wait_ge(sem, val))
        for e in ENGS:
            if s.cnt[e]:
                sem = s.sems[s.esem[e]]
                s.q["sync"].append(lambda E, sem=sem, val=s.cnt[e]: E.wait_ge(sem, val))


class Bump:
    def __init__(s, off, size):
        s.base, s.size, s.cur = off, size, off

    def reset(s):
        s.cur = s.base

    def take(s, nbytes, align=64):
        s.cur = (s.cur + align - 1) // align * align
        o = s.cur
        s.cur += nbytes
        assert s.cur <= s.base + s.size, ("region overflow", s.base, s.size, s.cur - s.base)
        return o


def _prod(x):
    r = 1
    for a in x:
        r *= a
    return r


class WQ:
    def __init__(s, k, items):
        s.k, s.items, s.loaded, s.nxt = k, items, [], 0
        s.cur = []
        s._fill()

    def _fill(s):
        k = s.k
        while s.nxt < len(s.items):
            src, shape = s.items[s.nxt]
            nb = _prod(shape) * 2
            out = sum(b_ - a_ for a_, b_ in k.w_out)
            if out + nb > WP_SZ - 8192:
                break
            lo = 0 if k.w_cur + nb > WP_SZ else k.w_cur
            if any(not (lo + nb <= a_ or lo >= b_) for a_, b_ in k.w_out):
                break
            s.loaded.append(k.w_load(src, BF16, shape))
            s.nxt += 1

    def get(s):
        if not s.loaded:
            s._fill()
        v = s.loaded.pop(0)
        s.cur.append(v)
        return v

    def done(s, v=None):
        if v is None:
            v = s.cur.pop(0)
        else:
            s.cur.remove(v)
        s.k.w_done(v)
        s._fill()


class K:
    def __init__(s, dbg=(), stages="all", stop=None):
        s.stop = stop
        s.dbg = list(dbg)
        s.stages = stages
        s.taps = []

    def sb(s, off_bytes, dt, shape, pn=128):
        item = ITEM[dt]
        assert off_bytes % item == 0
        h = {F32: s.AR32, BF16: s.AR16, I32: s.ARI}[dt]
        dims = []
        st = 1
        for c in reversed(shape):
            dims.insert(0, (st, c))
            st *= c
        assert off_bytes + st * item <= SB_TOTAL
        return V(h, "sb", SB_TOTAL // item, item, off_bytes // item, dims, 0, pn)

    def alloc(s, bump, dt, shape, pn=128):
        return s.sb(bump.take(_prod(shape) * ITEM[dt]), dt, shape, pn)

    def psv(s, bank, dt, shape, col_bytes=0, pn=128):
        item = ITEM[dt]
        h = s.PS32 if dt == F32 else s.PS16
        dims = []
        st = 1
        for c in reversed(shape):
            dims.insert(0, (st, c))
            st *= c
        off_b = bank * 2048 + col_bytes
        assert off_b + st * item <= PS_TOTAL
        return V(h, "ps", PS_TOTAL // item, item, off_b // item, dims, 0, pn)

    def ps_alloc(s, n=1, hold=False):
        for _ in range(16):
            if s.ps_cur + n > 8:
                s.ps_cur = 0
            b = s.ps_cur
            if any((b + k) in s.ps_res for k in range(n)):
                s.ps_cur = b + 1
                continue
            s.ps_cur = b + n
            if hold:
                for k in range(n):
                    s.ps_res.add(b + k)
            return b
        raise RuntimeError("psum ring: no free bank")

    def ps_release(s, b, n=1):
        for k in range(n):
            s.ps_res.discard(b + k)

    def dram(s, name, shape, dt, kind=None):
        if kind:
            h = s.nc.dram_tensor(name, list(shape), dt, kind=kind)
        else:
            h = s.nc.dram_tensor(name, list(shape), dt)
        dims = []
        st = 1
        for c in reversed(shape):
            dims.insert(0, (st, c))
            st *= c
        v = V(h, "dram", dims[0][0], ITEM[dt], 0, dims[1:], 0, dims[0][1], dkey=None)
        return v

    def dsel(s, v, lead_shape, idx, dkey=None):
        full = [(v.pstep, v.pn)] + list(v.dims)
        off = v.off
        if not isinstance(idx, tuple):
            idx = (idx,)
        for i, ix in enumerate(idx):
            off += full[i][0] * ix
        rest = full[len(idx):]
        return V(v.h, "dram", rest[0][0], v.item, off, rest[1:], 0, rest[0][1], dkey=dkey)

    def act(s, out, in_, func, scale=None, bias=None, accum=None):
        kw = {}
        ins = [in_]
        if scale is not None:
            if isinstance(scale, V):
                ins.append(scale)
                kw["scale"] = scale.ap()
            else:
                kw["scale"] = float(scale)
        if bias is not None:
            if isinstance(bias, V):
                ins.append(bias)
                kw["bias"] = bias.ap()
            else:
                kw["bias"] = float(bias)
        oa, ia = out.ap(), in_.ap()
        outs = [out]
        if accum is not None:
            kw["accum_out"] = accum.ap()
            outs.append(accum)
        s.p.op("scalar", lambda E: E.activation(out=oa, in_=ia, func=func, **kw), ins, outs)

    def tt(s, out, a, b, op, eng="vector"):
        oa, aa, ba = out.ap(), a.ap(), b.ap()
        s.p.op(eng, lambda E: E.tensor_tensor(out=oa, in0=aa, in1=ba, op=op), [a, b], [out])

    def ts(s, out, a, s1, op0, s2=None, op1=None, eng="vector"):
        ins = [a]
        oa, aa = out.ap(), a.ap()
        cv = lambda x: x.ap() if isinstance(x, V) else (None if x is None else (x if isinstance(x, int) else float(x)))
        x1, x2 = cv(s1), cv(s2)
        if isinstance(s1, V):
            ins.append(s1)
        if isinstance(s2, V):
            ins.append(s2)
        if op1 is None:
            s.p.op(eng, lambda E: E.tensor_scalar(out=oa, in0=aa, scalar1=x1, scalar2=None, op0=op0), ins, [out])
        else:
            s.p.op(eng, lambda E: E.tensor_scalar(out=oa, in0=aa, scalar1=x1, scalar2=x2, op0=op0, op1=op1), ins, [out])

    def stt(s, out, a, sc, b, op0, op1, eng="vector"):
        ins = [a, b]
        oa, aa, ba = out.ap(), a.ap(), b.ap()
        x = sc.ap() if isinstance(sc, V) else float(sc)
        if isinstance(sc, V):
            ins.append(sc)
        s.p.op(eng, lambda E: E.scalar_tensor_tensor(out=oa, in0=aa, scalar=x, in1=ba, op0=op0, op1=op1), ins, [out])

    def cp(s, out, in_, eng="vector"):
        oa, ia = out.ap(), in_.ap()
        if eng == "scalar":
            s.p.op("scalar", lambda E: E.activation(out=oa, in_=ia, func=AF.Identity), [in_], [out])
        else:
            s.p.op(eng, lambda E: E.tensor_copy(out=oa, in_=ia), [in_], [out])

    def memset(s, out, val, eng="vector"):
        oa = out.ap()
        s.p.op(eng, lambda E: E.memset(oa, float(val)), [], [out])

    def recip(s, out, in_):
        oa, ia = out.ap(), in_.ap()
        s.p.op("vector", lambda E: E.reciprocal(out=oa, in_=ia), [in_], [out])

    def scan(s, out, d0, d1, init, op0=ALU.mult, op1=ALU.add):
        ins = [d0, d1]
        oa, a0, a1 = out.ap(), d0.ap(), d1.ap()
        x = init.ap() if isinstance(init, V) else float(init)
        if isinstance(init, V):
            ins.append(init)
        s.p.op("vector", lambda E: E.tensor_tensor_scan(out=oa, data0=a0, data1=a1, initial=x, op0=op0, op1=op1), ins, [out])

    def mm(s, out, lhsT, rhs, start, stop, signal=None):
        if signal is None:
            signal = stop
        oa, la, ra = out.ap(), lhsT.ap(), rhs.ap()
        s.p.op("tensor", lambda E: E.matmul(oa, la, ra, start=start, stop=stop), [lhsT, rhs], [out], signal=signal)

    def trp(s, out, in_, ident, signal=True):
        oa, ia, da = out.ap(), in_.ap(), ident.ap()
        s.p.op("tensor", lambda E: E.transpose(out=oa, in_=ia, identity=da), [in_, ident], [out], signal=signal)

    def dma(s, out, in_, q="sync"):
        s.p.dma(q, out, in_)

    def tap(s, name, v, dt=F32):
        if name not in s.dbg:
            return
        shape = [v.pn] + v.shape
        d = s.dram("dbg_" + name, shape, dt, kind="ExternalOutput")
        s.taps.append(("dbg_" + name, shape))
        s.dma(d, v)

    def w_reset(s):
        s.w_cur = 0
        s.w_out = []

    def w_load(s, src, dt, shape):
        nb = _prod(shape) * ITEM[dt]
        if s.w_cur + nb > WP_SZ:
            s.w_cur = 0
        lo, hi = s.w_cur, s.w_cur + nb
        for (a, b) in s.w_out:
            assert hi <= a or lo >= b, "weight ring overrun (prefetch too deep)"
        s.w_out.append((lo, hi))
        s.w_cur = hi
        v = s.sb(WP_OFF + lo, dt, shape)
        v.wr = (lo, hi)
        s.dma(v, src, q="gpsimd")
        return v

    def w_done(s, v=None):
        if v is None:
            s.w_out.pop(0)
        else:
            s.w_out.remove(v.wr)

    def build(s):
        nc = s.nc = bass.Bass("TRN2", target_bir_lowering=False)
        s.es = es = ExitStack()
        I = lambda n, sh: s.dram(n, sh, F32, kind="ExternalInput")
        s.d_xown = I("xT_own", [128, NDK, 1024])
        s.d_xoth = I("xT_oth", [3, 128, NDK, 1024])
        s.d_xp = I("xT_p", [128, NDK, 512])
        s.d_cond = I("condT", [128, NDK, 2])
        s.d_vecs = I("vecs", [128, 392])
        s.d_s5small = I("s5small", [128, 3, 128])
        s.d_s5B = I("s5B", [128, 2, 128, 16])
        s.d_s5C = I("s5C", [128, 2, 128, 16])
        s.d_s5Dm = I("s5Dm", [128, 128])
        s.d_s5h0 = I("s5h0", [128, 2, 128])
        s.d_consts = I("consts", [128, 3, 128])
        s.d_wada = I("w_ada_s", [24, 128, NDK, 512])
        s.d_win = I("w_in_s", [128, 128, NDK, 128])
        s.d_wglu = I("w_glu_s", [16, 128, 16, 128])
        s.d_wos = I("w_os_s", [32, 128, 16, 128])
        s.d_wol = I("w_ol_s", [32, 128, 16, 128])
        s.d_wo = I("w_o_s", [32, 128, NDK, 128])
        s.d_wgate = I("w_gate", [16, 128, 4, 128])
        O = lambda n, sh: s.dram(n, sh, F32, kind="ExternalOutput")
        s.d_ys = O("yT_s", [128, NDK, 1024])
        s.d_yp = O("yT_p", [128, NDK, 512])
        s.d_stlru = O("st_lru", [128, 16, 2, 2])
        s.d_sts5 = O("st_s5", [128, 2, 2, 128])
        s.d_hts = s.dram("hT_spill", [128, NDK, 1024], BF16)
        s.d_tab = {n: s.dram("tab_" + n, [16, 128, 1024], BF16) for n in ("WTr", "WTi", "M", "Or", "Oi")}
        s.d_tabAT = s.dram("tab_AT", [16, 128, 192], F32)

        s.AR32 = es.enter_context(nc.sbuf_tensor("arena", [128, SB_TOTAL // 4], F32))
        s.AR16 = s.AR32.bitcast(BF16)
        s.ARI = s.AR32.bitcast(I32)
        s.PS32 = es.enter_context(nc.psum_tensor("psum", [128, PS_TOTAL // 4], F32))
        s.PS16 = s.PS32.bitcast(BF16)
        s.p = Prog(nc, es)
        s.ps_cur = 0
        s.ps_res = set()
        s.w_reset()
        s.cb = Bump(CONST_OFF, CONST_SZ)

        s.emit()

        s.p.finish()
        with nc.Block() as block:
            for e in ENGS:
                q = s.p.q[e]

                def run(E, q=q):
                    for f in q:
                        f(E)

                getattr(block, e)(run)
        es.close()
        return nc

    def emit(s):
        st = s.stages
        s.stage_consts()
        gm = s.stage_mod()
        gt = s.stage_tables()
        next(gt)
        for i in range(24):
            next(gm)
            if i % 3 == 2 or i >= 22:
                next(gt, None)
            if i >= 8:
                next(gt, None)
        for g in (gm, gt):
            for _ in g:
                pass
        if st == "pre":
            return
        cfgP = dict(name="P", NT=512, ntb=1, nseg=2, NCs=32, L=5, ci=1, conv=(2, 256), xd=s.d_xp, yd=s.d_yp)
        cfgS = dict(name="S", NT=1024, ntb=2, nseg=1, NCs=128, L=7, ci=0, conv=(16, 64), xd=s.d_xown, yd=s.d_ys)
        if st in ("P", "all"):
            s.run_pass(cfgP, full=True)
            s.dma(s.d_stlru, s.finl)
            s.dma(s.d_sts5, s.fin5)
        if st in ("S0", "all"):
            for o in range(3):
                c0 = dict(cfgS)
                c0["xd"] = s.dsel(s.d_xoth, None, o)
                s.run_pass(c0, full=False, o=o)
            s.stage_fold()
        if st in ("S", "all"):
            if st == "S":
                s.cp(s.hin5, s.s5h0)
                s.cp(s.hinl, s.c_h0lru)
            s.run_pass(cfgS, full=True)

    def run_pass(s, cfg, full, o=None):
        NT = cfg["NT"]
        hT = s.sb(R1_OFF, BF16, [NDK, NT])
        s.stage_hT(cfg, hT, spill=(full and cfg["name"] == "S"))
        s.tap("hT_" + cfg["name"], hT, BF16)
        if s.stop == "hT":
            return
        YaT = s.sb(R2_OFF, BF16, [16, NT])
        Ya3 = s.sb(R3_OFF, BF16, [16, NT])
        Yb = s.sb(R2_OFF, BF16, [16, NT])
        if not full:
            items = [(s.dsel(s.d_win, None, 32), [NDK, 128]), (s.dsel(s.d_win, None, 0), [NDK, 128])]
            for i in range(16):
                if i + 1 < 16:
                    items.append((s.dsel(s.d_win, None, i + 1), [NDK, 128]))
                items.append((s.dsel(s.d_wgate, None, i), [4, 128]))
                if i + 1 < 16:
                    items.append((s.dsel(s.d_win, None, 32 + i + 1), [NDK, 128]))
            wq = WQ(s, items)
            g5 = s.stage_s5_sum(cfg, hT, o, wq)
            gl = s.stage_lru(cfg, hT, Yb, full, o, wq=wq, ws_off=R2_OFF)
            next(gl)
            next(g5)
            for i in range(16):
                next(g5)
                next(gl)
            for g in (g5, gl):
                for _ in g:
                    pass
            return
        for _ in s.stage_s5(cfg, hT, YaT, full, o):
            pass
        if s.stop == "s5":
            s.tap("YaT_" + cfg["name"], YaT, BF16)
            return
        s.tap("YaT_" + cfg["name"], YaT, BF16)
        s.stage_glu(cfg, hT, YaT, Ya3)
        s.tap("Ya3_" + cfg["name"], Ya3, BF16)
        if s.stop == "glu":
            return
        for _ in s.stage_lru(cfg, hT, Yb, full, o):
            pass
        s.tap("Yb_" + cfg["name"], Yb, BF16)
        if s.stop == "lru":
            return
        s.stage_merge_out(cfg, Ya3, Yb)

    def stage_hT(s, cfg, hT, spill):
        NT, ntb, ci, xd = cfg["NT"], cfg["ntb"], cfg["ci"], cfg["xd"]
        b = Bump(R2_OFF, R2_SZ)
        xc = [s.alloc(b, F32, [2, NT]) for _ in range(3)]
        tmpf = [s.alloc(b, F32, [NT]) for _ in range(2)]
        b3 = Bump(R3_OFF, R3_SZ)
        sqs = [s.alloc(b3, F32, [2, NT]) for _ in range(2)]
        rs = s.sb(SCR_OFF, F32, [NT])
        pss = [s.ps_alloc(1) for _ in range(ntb)]
        for c in range(16):
            x = xc[c % 3]
            sq = sqs[c % 2]
            s.dma(x, xd[2 * c:2 * c + 2])
            s.act(sq, x, AF.Square)
            for j in range(2):
                for tb in range(ntb):
                    s.mm(s.psv(pss[tb], F32, [512]), s.ones32, sq[j, tb * 512:(tb + 1) * 512],
                         start=(c == 0 and j == 0), stop=(c == 15 and j == 1), signal=(j == 1 and tb == ntb - 1))
        for tb in range(ntb):
            r = rs[tb * 512:(tb + 1) * 512]
            s.ts(r, s.psv(pss[tb], F32, [512]), 1.0 / D, ALU.mult, EPS, ALU.add)
            s.act(r, r, AF.Sqrt)
            s.recip(r, r)
        for c in range(16):
            x = xc[(c + 1) % 3]
            s.dma(x, xd[2 * c:2 * c + 2])
            for j in range(2):
                dk = 2 * c + j
                t = tmpf[j]
                s.stt(t, x[j], s.gs[dk, ci:ci + 1], rs, ALU.mult, ALU.mult)
                s.act(hT[dk], t, AF.Identity, bias=s.sh[dk, ci:ci + 1])
        if spill:
            s.dma(s.d_hts, hT)

    def cmac(s, dst, src, ATk, t1, t2):
        n = src.shape[0]
        Ar = ATk[0].bc(0, 2).bc(0, n)
        A2b = ATk[1:3].bc(0, n)
        t1v, t2v = t1[0:n], t2[0:n]
        s.tt(t1v, src, Ar, ALU.mult)
        s.tt(t2v, src[:, ::-1, :], A2b, ALU.mult)
        s.tt(t1v, t1v, t2v, ALU.add)
        s.tt(dst, dst, t1v, ALU.add)

    def stage_s5(s, cfg, hT, YaT, full, o, wq=None):
        NT, nseg, NCs, L = cfg["NT"], cfg["nseg"], cfg["NCs"], cfg["L"]
        NCt = NT // 8
        isS = cfg["name"] == "S"
        b3 = Bump(R3_OFF, R3_SZ)
        tabs = []
        for sl in range(2):
            t = {n: s.alloc(b3, BF16, [8, 128]) for n in ("WTr", "WTi", "M", "Or", "Oi")}
            t["AT"] = s.alloc(b3, F32, [8, 3, 8])
            tabs.append(t)
        Uc2 = s.alloc(b3, BF16, [8, 8, 16])
        X = s.alloc(b3, BF16, [8, NCt])
        Hy = s.alloc(b3, BF16, [NCt, 2, 8])
        bs = Bump(SCR_OFF, SCR_SZ)
        Hs = s.alloc(bs, F32, [nseg, NCs, 2, 8])
        t1 = s.alloc(bs, F32, [max(NCs // 2, 1), 2, 8])
        t2 = s.alloc(bs, F32, [max(NCs // 2, 1), 2, 8])
        Yc = s.alloc(bs, F32, [8, 8, 16])
        g1 = s.alloc(bs, F32, [8, NCt])
        tnames = ("WTr", "WTi", "M", "Or", "Oi") if full else ("WTr", "WTi")
        if wq is None:
            wq = WQ(s, [(s.dsel(s.d_win, None, i), [NDK, 128]) for i in range(16)])

        def A1(i):
            tb_ = tabs[i % 2]
            for n in tnames:
                s.dma(tb_[n].merge(), s.dsel(s.d_tab[n], None, i, dkey=i))
            s.dma(tb_["AT"].merge(), s.dsel(s.d_tabAT, None, i, dkey=i))
            wua = wq.get()
            pb = s.ps_alloc(2)
            pp = s.psv(pb, F32, [8, 128]).P(0, NCt)
            for s_ in range(8):
                for dk in range(NDK):
                    s.mm(pp[s_], hT[dk, s_ * NCt:(s_ + 1) * NCt], wua[dk], start=(dk == 0), stop=(dk == NDK - 1))
            wq.done(wua)
            ppv = pp._new(dims=[(128, 8), (16, 8), (1, 16)])
            Ucv = Uc2.P(0, NCt).T(1, 0, 2)
            s.act(Ucv[0:4], ppv[0:4], AF.Identity)
            s.act(Ucv[4:8], ppv[4:8], AF.Identity)

        def A2(i):
            tb_ = tabs[i % 2]
            gsl = slice(i * 8, i * 8 + 8)
            pbT = s.ps_alloc(1)
            pT = s.psv(pbT, BF16, [8, NCt])
            Ucf = Uc2._new(dims=[(128, 8), (1, 128)]).P(0, NCt)
            idb = s.identb.P(0, NCt)[0:NCt]
            for g in range(8):
                s.trp(pT[g], Ucf[g], idb)
            s.cp(X, pT)
            nbk = max(1, (8 * NCt * 4) // 2048)
            for hg in range(2):
                pbS = s.ps_alloc(nbk)
                pS = s.psv(pbS, F32, [4, 2, NCt])
                for g in range(4):
                    for comp in range(2):
                        s.mm(pS[g, comp], tb_["WTr" if comp == 0 else "WTi"][hg * 4 + g], X[hg * 4 + g], True, True)
                gq = slice(hg * 4, hg * 4 + 4)
                for comp in range(2):
                    src = pS[:, comp, :]
                    dstf = Hs[:, :, comp, gq]._new(dims=[(16, NCt), (1, 4)])
                    s.act(dstf.P(0, 64), src.T(1, 0).P(0, 64), AF.Identity)
                    for sg in range(nseg):
                        srcb = pS[:, comp, sg * NCs:(sg + 1) * NCs].T(1, 0)
                        s.cp(Hs[sg, ::-1, comp, gq].P(64, 64), srcb.P(64, 64))
            if full and isS:
                s.cmac(Hs[0, 0:1], s.hin5[:, gsl]._new(dims=[(0, 1)] + s.hin5[:, gsl].dims), tb_["AT"][0], t1, t2)
            for sg in range(nseg):
                H = Hs[sg]
                for k in range(L):
                    d = 1 << k
                    s.cmac(H[2 * d - 1::2 * d], H[d - 1::2 * d], tb_["AT"][k], t1, t2)
                if not full:
                    continue
                for k in range(L - 2, -1, -1):
                    d = 1 << k
                    s.cmac(H[3 * d - 1::2 * d], H[2 * d - 1:NCs - 1:2 * d], tb_["AT"][k], t1, t2)
            if not full:
                s.cp(s.hl5[o][:, gsl], Hs[0, NCs - 1])
            elif not isS:
                for sg in range(nseg):
                    s.cp(s.fin5[sg][:, gsl], Hs[sg, NCs - 1])

        def B(i):
            tb_ = tabs[i % 2]
            gsl = slice(i * 8, i * 8 + 8)
            for sg in range(nseg):
                base = sg * NCs
                s.act(Hy[base + 1:base + NCs].P(0, 64), Hs[sg, 0:NCs - 1].P(0, 64), AF.Identity)
                s.cp(Hy[base:base + NCs - 1].P(64, 64), Hs[sg, NCs - 2::-1].P(64, 64))
                if isS:
                    s.cp(Hy[base].P(0, 64), s.hin5[:, gsl].P(0, 64))
                    s.cp(Hy[base + NCs - 1].P(64, 64), s.hin5[:, gsl].P(64, 64))
                else:
                    s.memset(Hy[base].P(0, 64), 0.0)
                    s.memset(Hy[base + NCs - 1].P(64, 64), 0.0)
            pbY = s.ps_alloc(2)
            pY = s.psv(pbY, F32, [8, 128]).P(0, NCt)
            for g in range(8):
                s.mm(pY[g], X[g], tb_["M"][g], True, False)
                s.mm(pY[g], Hy[:, 0, g], tb_["Or"][g], False, False)
                s.mm(pY[g], Hy[:, 1, g], tb_["Oi"][g], False, True)
            pYv = pY._new(dims=[(128, 8), (16, 8), (1, 16)])
            Ycv = Yc.P(0, NCt).T(1, 0, 2)
            s.act(Ycv[0:4], pYv[0:4], AF.Identity)
            s.cp(Ycv[4:8], pYv[4:8])
            nbt = max(1, (8 * NCt * 4) // 2048)
            pb2 = s.ps_alloc(nbt)
            pY2 = s.psv(pb2, F32, [8, NCt])
            Ycf = Yc._new(dims=[(128, 8), (1, 128)]).P(0, NCt)
            idf = s.identf.P(0, NCt)[0:NCt]
            for t in range(8):
                s.trp(pY2[t], Ycf[t], idf)
            yv = YaT[i]._new(dims=[(NCt, 8), (1, NCt)])
            tper = 8 // nbt
            for hb in range(nbt):
                ts_ = slice(hb * tper, (hb + 1) * tper)
                src, gg, out = pY2[ts_], g1[ts_], yv[ts_]
                s.act(gg, src, AF.Square)
                s.ts(gg, gg, 0.044715, ALU.mult, 1.0, ALU.add)
                s.tt(gg, gg, src, ALU.mult)
                s.act(gg, gg, AF.Sigmoid, scale=1.5957691216057308)
                s.tt(out, gg, src, ALU.mult)

        for i in range(16):
            A1(i)
            if full and i > 0:
                B(i - 1)
            A2(i)
            yield
        if full:
            B(15)

    def stage_s5_sum(s, cfg, hT, o, wq):
        NT, NCs, L = cfg["NT"], cfg["NCs"], cfg["L"]
        NCt = NT // 8
        b3 = Bump(R3_OFF, R3_SZ)
        tabs = []
        for sl in range(3):
            t = {n: s.alloc(b3, BF16, [8, 128]) for n in ("WTr", "WTi")}
            t["AT"] = s.alloc(b3, F32, [8, 3, 8])
            tabs.append(t)
        Uc2s = [s.alloc(b3, BF16, [8, 8, 16]) for _ in range(2)]
        Xs = [s.alloc(b3, BF16, [8, NCt]) for _ in range(2)]
        bs = Bump(SCR_OFF, SCR_SZ)
        Hs = s.alloc(bs, F32, [1, NCs, 2, 8])
        t1 = s.alloc(bs, F32, [NCs // 2, 2, 8])
        t2 = s.alloc(bs, F32, [NCs // 2, 2, 8])

        def S1(i):
            tb_ = tabs[i % 3]
            for n in ("WTr", "WTi"):
                s.dma(tb_[n].merge(), s.dsel(s.d_tab[n], None, i, dkey=i))
            s.dma(tb_["AT"].merge(), s.dsel(s.d_tabAT, None, i, dkey=i))
            wua = wq.get()
            pb = s.ps_alloc(2)
            pp = s.psv(pb, F32, [8, 128]).P(0, NCt)
            for s_ in range(8):
                for dk in range(NDK):
                    s.mm(pp[s_], hT[dk, s_ * NCt:(s_ + 1) * NCt], wua[dk], start=(dk == 0), stop=(dk == NDK - 1))
            wq.done(wua)
            ppv = pp._new(dims=[(128, 8), (16, 8), (1, 16)])
            Ucv = Uc2s[i % 2].P(0, NCt).T(1, 0, 2)
            s.act(Ucv[0:4], ppv[0:4], AF.Identity)
            s.act(Ucv[4:8], ppv[4:8], AF.Identity)

        def S2(i):
            pbT = s.ps_alloc(1)
            pT = s.psv(pbT, BF16, [8, NCt])
            Ucf = Uc2s[i % 2]._new(dims=[(128, 8), (1, 128)]).P(0, NCt)
            idb = s.identb.P(0, NCt)[0:NCt]
            for g in range(8):
                s.trp(pT[g], Ucf[g], idb)
            s.cp(Xs[i % 2], pT)

        def S3(i):
            tb_ = tabs[i % 3]
            X = Xs[i % 2]
            gsl = slice(i * 8, i * 8 + 8)
            nbk = max(1, (8 * NCt * 4) // 2048)
            for hg in range(2):
                pbS = s.ps_alloc(nbk)
                pS = s.psv(pbS, F32, [4, 2, NCt])
                for g in range(4):
                    for comp in range(2):
                        s.mm(pS[g, comp], tb_["WTr" if comp == 0 else "WTi"][hg * 4 + g], X[hg * 4 + g], True, True)
                gq = slice(hg * 4, hg * 4 + 4)
                for comp in range(2):
                    src = pS[:, comp, :]
                    dstf = Hs[:, :, comp, gq]._new(dims=[(16, NCt), (1, 4)])
                    s.act(dstf.P(0, 64), src.T(1, 0).P(0, 64), AF.Identity)
                    s.cp(Hs[0, ::-1, comp, gq].P(64, 64), pS[:, comp, :].T(1, 0).P(64, 64))
            H = Hs[0]
            for k in range(L):
                d = 1 << k
                s.cmac(H[2 * d - 1::2 * d], H[d - 1::2 * d], tb_["AT"][k], t1, t2)
            s.cp(s.hl5[o][:, gsl], Hs[0, NCs - 1])

        S1(0)
        yield
        for i in range(16):
            if i + 1 < 16:
                S1(i + 1)
            S2(i)
            if i >= 1:
                S3(i - 1)
            yield
        S3(15)

    def stage_glu(s, cfg, hT, YaT, Ya3):
        ntb = cfg["ntb"]
        bs = Bump(SCR_OFF, SCR_SZ)
        sg = s.alloc(bs, F32, [512]); sz = s.alloc(bs, F32, [512]); tt_ = s.alloc(bs, F32, [512])
        items = []
        for j in range(16):
            items.append((s.dsel(s.d_wglu, None, j), [16, 128]))
            items.append((s.dsel(s.d_win, None, 16 + j), [NDK, 128]))
        wq = WQ(s, items)
        for j in range(16):
            wg = wq.get()
            wz = wq.get()
            for tb in range(ntb):
                tsl = slice(tb * 512, (tb + 1) * 512)
                pg = s.psv(s.ps_alloc(1), F32, [512])
                for k in range(16):
                    s.mm(pg, wg[k], YaT[k, tsl], k == 0, k == 15)
                pz = s.psv(s.ps_alloc(1), F32, [512])
                for k in range(NDK):
                    s.mm(pz, wz[k], hT[k, tsl], k == 0, k == NDK - 1)
                s.act(sg, pg, AF.Sigmoid, bias=s.c_bglu[j:j + 1])
                s.act(sz, pz, AF.Silu)
                s.tt(tt_, YaT[j, tsl], sg, ALU.mult)
                s.tt(Ya3[j, tsl], tt_, sz, ALU.mult)
            wq.done()
            wq.done()

    def stage_lru(s, cfg, hT, Yb, full, o, wq=None, ws_off=SCR_OFF):
        NT, ntb, nseg = cfg["NT"], cfg["ntb"], cfg["nseg"]
        R, Lr = cfg["conv"]
        isS = cfg["name"] == "S"
        SL = NT // nseg
        RB = 512 // Lr
        bs = Bump(ws_off, R2_SZ if ws_off == R2_OFF else SCR_SZ)
        racc = s.alloc(bs, F32, [2])
        rsum = s.alloc(bs, F32, [1])
        xb = s.alloc(bs, F32, [NT]); xbb = s.alloc(bs, BF16, [NT])
        a = s.alloc(bs, F32, [NT]); ib = s.alloc(bs, F32, [NT]); hf = s.alloc(bs, F32, [NT]); hb = s.alloc(bs, F32, [NT])
        u = hb
        both = (not full) or (not isS)
        if both:
            a1 = s.alloc(bs, F32, [NT]); ib1 = s.alloc(bs, F32, [NT])
            racc2 = s.alloc(bs, F32, [2, 2]); rsum2 = s.alloc(bs, F32, [2])
        NCt = NT // 8
        spb = 512 // NCt
        nat = lambda v, tb: v._new(off=v.off + tb * spb, dims=[(1, spb), (8, NCt)])
        if wq is None:
            items = []
            for j in range(16):
                items.append((s.dsel(s.d_win, None, 32 + j), [NDK, 128]))
                items.append((s.dsel(s.d_wgate, None, j), [4, 128]))
                if full:
                    items.append((s.dsel(s.d_win, None, 48 + j), [NDK, 128]))
            wq = WQ(s, items)
        zero_bc = s.zero1._new(dims=[(0, NT)])
        T = lambda tb: slice(tb * 512, (tb + 1) * 512)

        def proj_u(j):
            wub = wq.get()
            banks = []
            for tb in range(ntb):
                bk = s.ps_alloc(1, hold=True)
                banks.append(bk)
                pu = s.psv(bk, F32, [512])
                for k in range(NDK):
                    s.mm(pu, wub[k], hT[k, T(tb)], k == 0, k == NDK - 1)
            wq.done(wub)
            return banks

        def conv(j, banks):
            for tb in range(ntb):
                pu = s.psv(banks[tb], F32, [spb, NCt])
                s.act(nat(u, tb), pu, AF.Identity)
                s.ps_release(banks[tb])
            s.act(xb, u, AF.Identity, scale=s.c_convw[2, j:j + 1], bias=s.c_convb[j:j + 1])
            u3 = u._new(dims=[(Lr, R), (1, Lr)])
            xb3 = xb._new(dims=[(Lr, R), (1, Lr)])
            for k, off in ((0, -2), (1, -1), (3, 1)):
                lo, hi = max(0, -off), Lr - max(0, off)
                s.stt(xb3[:, lo:hi], u3[:, lo + off:hi + off], s.c_convw[k, j:j + 1], xb3[:, lo:hi], ALU.mult, ALU.add)
            s.act(xbb, xb, AF.Identity)

        def gates(j, wgt, d):
            tmpb = hf if d == 0 else hb
            prs = []
            for tb in range(ntb):
                pr = s.psv(s.ps_alloc(1), F32, [512])
                s.mm(pr, wgt[d * 2 + 0], xbb[T(tb)], True, True)
                pi = s.psv(s.ps_alloc(1), F32, [512])
                s.mm(pi, wgt[d * 2 + 1], xbb[T(tb)], True, True)
                prs.append((pr, pi))
            for tb in range(ntb):
                pr, pi = prs[tb]
                s.act(a[T(tb)], pr, AF.Sigmoid, bias=s.c_br[d, j:j + 1], accum=(None if full else racc[tb:tb + 1]))
                s.act(ib[T(tb)], pi, AF.Sigmoid, bias=s.c_bi[d, j:j + 1])
            if not full:
                if ntb == 2:
                    s.tt(rsum, racc[0:1], racc[1:2], ALU.add)
                else:
                    s.cp(rsum, racc[0:1])
                s.act(s.pll[o, d, j:j + 1], rsum, AF.Exp, scale=s.nsp8[d, j:j + 1])
            if LRU_POLY:
                for tb in range(ntb):
                    tmp = tmpb[T(tb)]
                    cO = s.pcO[d, j]
                    s.ts(tmp, a[T(tb)], cO[4:5], ALU.mult)
                    for n in (3, 2, 1, 0):
                        s.stt(tmp, tmp, cO[n:n + 1], a[T(tb)], ALU.add, ALU.mult)
                    s.ts(tmp, tmp, 0.0, ALU.max)
                    s.act(tmp, tmp, AF.Sqrt)
                    s.tt(ib[T(tb)], ib[T(tb)], xb[T(tb)], ALU.mult)
                    s.tt(ib[T(tb)], ib[T(tb)], tmp, ALU.mult)
                    cA = s.pcA[d, j]
                    s.ts(tmp, a[T(tb)], cA[3:4], ALU.mult)
                    for n in (2, 1, 0):
                        s.stt(tmp, tmp, cA[n:n + 1], a[T(tb)], ALU.add, ALU.mult)
                    s.ts(a[T(tb)], tmp, 1.0, ALU.add)
            else:
                for tb in range(ntb):
                    s.act(tmpb[T(tb)], a[T(tb)], AF.Exp, scale=s.nsp16[d, j:j + 1])
                    s.act(a[T(tb)], a[T(tb)], AF.Exp, scale=s.nsp8[d, j:j + 1])
                for tb in range(ntb):
                    s.ts(tmpb[T(tb)], tmpb[T(tb)], 1.0, ALU.min)
                    s.tt(ib[T(tb)], ib[T(tb)], xb[T(tb)], ALU.mult)
                for tb in range(ntb):
                    s.act(tmpb[T(tb)], tmpb[T(tb)], AF.Sqrt, scale=-1.0, bias=1.0)
                for tb in range(ntb):
                    s.tt(ib[T(tb)], ib[T(tb)], tmpb[T(tb)], ALU.mult)
            for sg in range(nseg):
                ssl = slice(sg * SL, (sg + 1) * SL)
                init = (s.hinl[d, j:j + 1] if (full and isS) else 0.0)
                if d == 0:
                    s.scan(hf[ssl], a[ssl], ib[ssl], init)
                else:
                    s.scan(hb[ssl][::-1], a[ssl][::-1], ib[ssl][::-1], init)
                if full and not isS:
                    fin = hf[sg * SL + SL - 1:sg * SL + SL] if d == 0 else hb[sg * SL:sg * SL + 1]
                    s.cp(s.finl[j, sg, d:d + 1], fin)
            if not full:
                fin = hf[NT - 1:NT] if d == 0 else hb[0:1]
                s.cp(s.hll[o, d, j:j + 1], fin)

        def gates2(j, wgt):
            A_ = [a, a1]; IB = [ib, ib1]; TM = [hf, hb]
            for tb in range(ntb):
                ps = []
                for d in range(2):
                    pr = s.psv(s.ps_alloc(1), F32, [512])
                    s.mm(pr, wgt[d * 2 + 0], xbb[T(tb)], True, True)
                    pi = s.psv(s.ps_alloc(1), F32, [512])
                    s.mm(pi, wgt[d * 2 + 1], xbb[T(tb)], True, True)
                    ps.append((pr, pi))
                for d in range(2):
                    pr, pi = ps[d]
                    s.act(A_[d][T(tb)], pr, AF.Sigmoid, bias=s.c_br[d, j:j + 1], accum=(None if full else racc2[d, tb:tb + 1]))
                    s.act(IB[d][T(tb)], pi, AF.Sigmoid, bias=s.c_bi[d, j:j + 1])
            if not full:
                for d in range(2):
                    if ntb == 2:
                        s.tt(rsum2[d:d + 1], racc2[d, 0:1], racc2[d, 1:2], ALU.add)
                    else:
                        s.cp(rsum2[d:d + 1], racc2[d, 0:1])
            for d in range(2):
                if not full:
                    s.act(s.pll[o, d, j:j + 1], rsum2[d:d + 1], AF.Exp, scale=s.nsp8[d, j:j + 1])
                for tb in range(ntb):
                    s.act(TM[d][T(tb)], A_[d][T(tb)], AF.Exp, scale=s.nsp16[d, j:j + 1])
                    s.act(A_[d][T(tb)], A_[d][T(tb)], AF.Exp, scale=s.nsp8[d, j:j + 1])
            for d in range(2):
                for tb in range(ntb):
                    s.ts(TM[d][T(tb)], TM[d][T(tb)], 1.0, ALU.min)
                    s.tt(IB[d][T(tb)], IB[d][T(tb)], xb[T(tb)], ALU.mult)
            for d in range(2):
                for tb in range(ntb):
                    s.act(TM[d][T(tb)], TM[d][T(tb)], AF.Sqrt, scale=-1.0, bias=1.0)
            for d in range(2):
                for tb in range(ntb):
                    s.tt(IB[d][T(tb)], IB[d][T(tb)], TM[d][T(tb)], ALU.mult)
            if not full:
                s.scan(hf, a, ib, 0.0)
                s.cp(s.hll[o, 0, j:j + 1], hf[NT - 1:NT])
                s.scan(hb[::-1], a1[::-1], ib1[::-1], 0.0)
                s.cp(s.hll[o, 1, j:j + 1], hb[0:1])
            else:
                for sg in range(nseg):
                    ssl = slice(sg * SL, (sg + 1) * SL)
                    s.scan(hf[ssl], a[ssl], ib[ssl], 0.0)
                    s.cp(s.finl[j, sg, 0:1], hf[sg * SL + SL - 1:sg * SL + SL])
                    s.scan(hb[ssl][::-1], a1[ssl][::-1], ib1[ssl][::-1], 0.0)
                    s.cp(s.finl[j, sg, 1:2], hb[sg * SL:sg * SL + 1])

        banks = proj_u(0)
        yield
        for j in range(16):
            wgt = wq.get()
            zw = wq.get() if full else None
            nxt = proj_u(j + 1) if j + 1 < 16 else None
            conv(j, banks)
            if not both:
                gates(j, wgt, 0)
                gates(j, wgt, 1)
            else:
                gates2(j, wgt)
            wq.done(wgt)
            if full:
                s.tt(hf, hf, hb, ALU.add)
                for tb in range(ntb):
                    pz = s.psv(s.ps_alloc(1), F32, [512])
                    for k in range(NDK):
                        s.mm(pz, zw[k], hT[k, T(tb)], k == 0, k == NDK - 1)
                    s.act(a[T(tb)], pz, AF.Silu)
                    pv = lambda v: v._new(dims=[(NCt, spb), (1, NCt)])
                    s.tt(pv(Yb[j, T(tb)]), nat(hf, tb), pv(a[T(tb)]), ALU.mult)
                wq.done(zw)
            banks = nxt
            yield

    def stage_fold(s):
        bs = Bump(SCR_OFF, SCR_SZ)
        t = s.alloc(bs, F32, [16])
        s.cp(s.hinl, s.c_h0lru)
        for d, order, m0 in ((0, (0, 1, 2), 0), (1, (2, 1, 0), 3)):
            h = s.hinl[d]
            for o in order:
                s.tt(t, s.pll[o, d], h, ALU.mult)
                s.tt(t, t, s.hll[o, d], ALU.add)
                s.tt(t, t, h, ALU.subtract)
                s.stt(h, t, s.c_masks[m0 + o:m0 + o + 1], h, ALU.mult, ALU.add)
        t1 = s.alloc(bs, F32, [2, 128]); t2 = s.alloc(bs, F32, [2, 128])
        s.cp(s.hin5, s.s5h0)
        Pr = s.PT[0].bc(0, 2)
        P2b = s.PT[1:3]
        for pr, order, m0 in (((0, 64), (0, 1, 2), 0), ((64, 64), (2, 1, 0), 3)):
            h = s.hin5.P(*pr)
            for o in order:
                a_, b_ = t1.P(*pr), t2.P(*pr)
                s.tt(a_, h, Pr.P(*pr), ALU.mult)
                s.tt(b_, h[::-1], P2b.P(*pr), ALU.mult)
                s.tt(a_, a_, b_, ALU.add)
                s.tt(a_, a_, s.hl5[o].P(*pr), ALU.add)
                s.tt(a_, a_, h, ALU.subtract)
                s.stt(h, a_, s.c_masks[m0 + o:m0 + o + 1].P(*pr), h, ALU.mult, ALU.add)
        s.tap("hin5", s.hin5)
        s.tap("hinl", s.hinl)

    def stage_merge_out(s, cfg, Ya3, Yb):
        NT, ntb, ci, xd, yd = cfg["NT"], cfg["ntb"], cfg["ci"], cfg["xd"], cfg["yd"]
        isS = cfg["name"] == "S"
        hTb = s.sb(R1_OFF, BF16, [NDK, 512])
        mg = s.sb(R1_OFF + 32768, BF16, [NDK, 512])
        bs = Bump(SCR_OFF, SCR_SZ)
        sa = s.alloc(bs, F32, [512]); sb_ = s.alloc(bs, F32, [512]); t1 = s.alloc(bs, F32, [512]); t2 = s.alloc(bs, F32, [512])
        xt = [s.alloc(bs, F32, [512]) for _ in range(2)]
        xn = [s.alloc(bs, F32, [512]) for _ in range(2)]
        sq = s.alloc(bs, F32, [512]); rs = s.alloc(bs, F32, [512])
        for tb in range(ntb):
            tsl = slice(tb * 512, (tb + 1) * 512)
            if isS and tb == 0:
                s.dma(hTb, s.d_hts[:, tsl])
            items = []
            for m in range(NDK):
                items += [(s.dsel(s.d_wos, None, m), [16, 128]), (s.dsel(s.d_win, None, 64 + m), [NDK, 128]),
                          (s.dsel(s.d_wol, None, m), [16, 128]), (s.dsel(s.d_win, None, 96 + m), [NDK, 128])]
            for dt in range(NDK):
                items.append((s.dsel(s.d_wo, None, dt), [NDK, 128]))
            wq = WQ(s, items)
            for m in range(NDK):
                wos = wq.get()
                pya = s.psv(s.ps_alloc(1), F32, [512])
                for k in range(16):
                    s.mm(pya, wos[k], Ya3[k, tsl], k == 0, k == 15)
                wq.done()
                wga = wq.get()
                pga = s.psv(s.ps_alloc(1), F32, [512])
                for k in range(NDK):
                    s.mm(pga, wga[k], hTb[k], k == 0, k == NDK - 1)
                wq.done()
                wol = wq.get()
                pyb = s.psv(s.ps_alloc(1), F32, [512])
                for k in range(16):
                    s.mm(pyb, wol[k], Yb[k, tsl], k == 0, k == 15)
                wq.done()
                wgb = wq.get()
                pgb = s.psv(s.ps_alloc(1), F32, [512])
                for k in range(NDK):
                    s.mm(pgb, wgb[k], hTb[k], k == 0, k == NDK - 1)
                wq.done()
                s.act(sa, pga, AF.Sigmoid)
                s.act(sb_, pgb, AF.Sigmoid)
                s.tt(t1, sa, pya, ALU.mult)
                s.tt(t2, sb_, pyb, ALU.mult)
                s.tt(mg[m], t1, t2, ALU.add)
            if tb == 0:
                s.tap("mg_" + cfg["name"], mg, BF16)
            if isS and tb + 1 < ntb:
                s.dma(hTb, s.d_hts[:, (tb + 1) * 512:(tb + 2) * 512])
            last = (tb == ntb - 1)
            xres = s.sb(R2_OFF, F32, [NDK, 512]) if last else None
            pss = s.ps_alloc(1, hold=True)
            pssv = s.psv(pss, F32, [512])
            for dt in range(NDK):
                wo = wq.get()
                po = s.psv(s.ps_alloc(1), F32, [512])
                for k in range(NDK):
                    s.mm(po, wo[k], mg[k], k == 0, k == NDK - 1)
                wq.done()
                x_, n_ = xt[dt % 2], (xres[dt] if last else xn[dt % 2])
                s.dma(x_, xd[dt, tsl])
                s.stt(n_, po, s.gt[dt, ci:ci + 1], x_, ALU.mult, ALU.add)
                s.act(sq, n_, AF.Square)
                s.mm(pssv, s.ones32, sq, dt == 0, dt == NDK - 1, signal=True)
                if not last:
                    s.dma(yd[dt, tsl].key((dt, tb)), n_)
            s.ts(rs, pssv, 1.0 / D, ALU.mult, EPS, ALU.add)
            s.ps_release(pss)
            s.act(rs, rs, AF.Sqrt)
            s.recip(rs, rs)
            if last:
                for dt in range(NDK):
                    s.stt(xres[dt], xres[dt], s.c_fg[dt:dt + 1], rs, ALU.mult, ALU.mult)
                    if dt % 8 == 7:
                        s.dma(yd[dt - 7:dt + 1, tsl].key(("o", dt // 8, tb)), xres[dt - 7:dt + 1])
                continue
            s.dma(xt[0], yd[0, tsl].key((0, tb)))
            for dt in range(NDK):
                x_, n_ = xt[dt % 2], xn[dt % 2]
                if dt + 1 < NDK:
                    s.dma(xt[(dt + 1) % 2], yd[dt + 1, tsl].key((dt + 1, tb)))
                s.stt(n_, x_, s.c_fg[dt:dt + 1], rs, ALU.mult, ALU.mult)
                s.dma(yd[dt, tsl].key((dt, tb)), n_)

    def stage_consts(s):
        cb = s.cb
        A = lambda dt, sh: s.alloc(cb, dt, sh)
        s.vecs = A(F32, [392])
        s.dma(s.vecs, s.d_vecs)
        o = 0

        def take(n, shape=None):
            nonlocal o
            v = s.vecs[o:o + n]
            o += n
            if shape:
                dims = []
                stt_ = 1
                for c in reversed(shape):
                    dims.insert(0, (stt_, c))
                    stt_ *= c
                v = v._new(dims=dims)
            return v

        s.c_bada = take(96)
        s.c_ng = take(32)
        s.c_fg = take(32)
        s.c_bglu = take(16)
        s.c_convw = take(64, [4, 16])
        s.c_convb = take(16)
        s.c_br = take(32, [2, 16])
        s.c_bi = take(32, [2, 16])
        s.c_lam = take(32, [2, 16])
        s.c_h0lru = take(32, [2, 16])
        s.c_masks = take(6)
        s.identf = A(F32, [128])
        s.dma(s.identf, s.d_consts[0])
        s.identb = A(BF16, [128])
        s.cp(s.identb, s.identf)
        s.ones32 = A(F32, [128])
        s.memset(s.ones32, 1.0)
        s.condT = A(F32, [NDK, 2])
        s.dma(s.condT, s.d_cond)
        s.sc = A(BF16, [NDK, 2])
        s.act(s.sc, s.condT, AF.Silu)
        s.modv = A(F32, [96, 2])
        s.gs = A(F32, [NDK, 2])
        s.sh = s.modv[0:32]
        s.gt = s.modv[64:96]
        s.nsp8 = A(F32, [2, 16])
        s.nsp16 = A(F32, [2, 16])
        ep = A(F32, [2, 16]); t = A(F32, [2, 16]); t2_ = A(F32, [2, 16]); msk = A(F32, [2, 16])
        s.act(ep, s.c_lam, AF.Exp, scale=-1.0)
        s.act(t, ep, AF.Ln, scale=1.0, bias=1.0)
        s.ts(t2_, ep, 1.0 / 3.0, ALU.mult, -0.5, ALU.add)
        s.tt(t2_, t2_, ep, ALU.mult)
        s.ts(t2_, t2_, 1.0, ALU.add)
        s.tt(t2_, t2_, ep, ALU.mult)
        s.ts(msk, ep, 0.03, ALU.is_lt)
        s.tt(t2_, t2_, t, ALU.subtract)
        s.tt(t2_, t2_, msk, ALU.mult)
        s.tt(t, t, t2_, ALU.add)
        s.ts(s.nsp8, t, -8.0, ALU.mult)
        s.ts(s.nsp16, t, -16.0, ALU.mult)
        s.pcO = A(F32, [2, 16, 5])
        s.pcA = A(F32, [2, 16, 4])
        pw = A(F32, [2, 16])
        s.cp(pw, s.nsp16)
        f = 1.0
        for n in range(1, 6):
            f *= n
            s.ts(s.pcO[:, :, n - 1], pw, -1.0 / f, ALU.mult)
            if n < 5:
                s.tt(pw, pw, s.nsp16, ALU.mult)
        s.cp(pw, s.nsp8)
        f = 1.0
        for n in range(1, 5):
            f *= n
            s.ts(s.pcA[:, :, n - 1], pw, 1.0 / f, ALU.mult)
            if n < 4:
                s.tt(pw, pw, s.nsp8, ALU.mult)
        s.s5h0 = A(F32, [2, 128])
        s.dma(s.s5h0, s.d_s5h0)
        s.hin5 = A(F32, [2, 128])
        s.hl5 = A(F32, [3, 2, 128])
        s.PT = A(F32, [3, 128])
        s.hinl = A(F32, [2, 16])
        s.hll = A(F32, [3, 2, 16])
        s.pll = A(F32, [3, 2, 16])
        s.finl = A(F32, [16, 2, 2])
        s.fin5 = A(F32, [2, 2, 128])
        s.zero1 = A(F32, [1])
        s.memset(s.zero1, 0.0)

    def stage_mod(s):
        pm_bank = s.ps_alloc(1, hold=True)
        pm = s.psv(pm_bank, F32, [96, 2])
        for i in range(24):
            slot = s.sb(R1_OFF + (i % 2) * 32768, BF16, [NDK, 512])
            s.dma(slot, s.dsel(s.d_wada, None, i), q="gpsimd")
            for ct in range(4):
                for dk in range(NDK):
                    first = (i == 0 and ct == 0 and dk == 0)
                    last = (i == 23 and ct == 3 and dk == NDK - 1)
                    s.mm(pm[i * 4 + ct], slot[dk, ct * 128:(ct + 1) * 128], s.sc[dk], start=first, stop=last,
                         signal=(dk == NDK - 1 and ct == 3))
            yield
        s.tt(s.modv, pm, s.c_bada.bc(1, 2), ALU.add)
        s.ps_release(pm_bank)
        s.ts(s.gs, s.modv[32:64], 1.0, ALU.add)
        s.tt(s.gs, s.gs, s.c_ng.bc(1, 2), ALU.mult)
        s.tap("modv", s.modv)
        s.tap("gs", s.gs)

    def cmul(s, out2, x2, Er, Ei, t1, t2, pr=None):
        xs = x2[::-1]
        Erb = Er.bc(0, 2)
        Eib = Ei.bc(0, 2)
        vs = [out2, x2, xs, t1, t2, Erb, Eib]
        if pr is not None:
            vs = [v.P(*pr) for v in vs]
        out2, x2, xs, t1, t2, Erb, Eib = vs
        s.tt(t1, x2, Erb, ALU.mult)
        s.tt(t2, xs, Eib, ALU.mult)
        s.tt(out2[0], t1[0], t2[0], ALU.subtract)
        s.tt(out2[1], t1[1], t2[1], ALU.add)

    def stage_tables(s):
        b = Bump(R2_OFF, SB_TOTAL - R2_OFF)
        A = lambda dt, sh: s.alloc(b, dt, sh)
        small = A(F32, [3, 128])
        s.dma(small, s.d_s5small)
        masks = A(F32, [2, 128])
        s.dma(masks, s.d_consts[1:3])
        s.maskF, s.maskB = masks[0], masks[1]
        lamr, lami, lstep = small[0], small[1], small[2]
        dt_ = A(F32, [128]); ar = A(F32, [128]); th = A(F32, [128]); em1 = A(F32, [128]); rho = A(F32, [128])
        kf = A(F32, [128]); ki = s.alloc(b, I32, [128]); s2 = A(F32, [128]); c2 = A(F32, [128])
        sn = A(F32, [128]); nr = A(F32, [128]); tq = A(F32, [128]); tq2 = A(F32, [128])
        L2 = A(F32, [2, 128])
        beta2 = A(F32, [2, 128])
        betar, betai = beta2[0], beta2[1]
        s.ts(ki, lstep, 1.0 / float(np.log(2.0)), ALU.mult)
        s.cp(kf, ki)
        s.stt(tq, kf, -0.693359375, lstep, ALU.mult, ALU.add)
        s.stt(tq, kf, 2.12194440e-4, tq, ALU.mult, ALU.add)
        s.ts(dt_, tq, 1.0 / 9.0, ALU.mult, 1.0, ALU.add)
        for cst in (1.0 / 8.0, 1.0 / 7.0, 1.0 / 6.0, 1.0 / 5.0, 1.0 / 4.0, 1.0 / 3.0, 1.0 / 2.0, 1.0):
            s.tt(dt_, dt_, tq, ALU.mult)
            s.ts(dt_, dt_, cst, ALU.mult, 1.0, ALU.add)
        s.ts(ki, ki, 127.0, ALU.add)
        s.ts(ki, ki, 23, ALU.logical_shift_left)
        kpow = s.sb(ki.off * 4, F32, [128])
        s.tt(dt_, dt_, kpow, ALU.mult)
        s.tap("dt", dt_)
        s.tt(ar, lamr, dt_, ALU.mult)
        s.tt(th, lami, dt_, ALU.mult)
        s.ts(em1, ar, 1.0 / 6.0, ALU.mult, 1.0, ALU.add)
        for cst in (1.0 / 5.0, 1.0 / 4.0, 1.0 / 3.0, 1.0 / 2.0):
            s.tt(em1, em1, ar, ALU.mult)
            s.ts(em1, em1, cst, ALU.mult, 1.0, ALU.add)
        s.tt(em1, em1, ar, ALU.mult)
        s.ts(rho, em1, 1.0, ALU.add)
        s.ts(ki, th, 1.0 / (2.0 * np.pi), ALU.mult)
        s.cp(kf, ki)
        s.stt(th, kf, -2.0 * np.pi, th, ALU.mult, ALU.add)
        s.act(s2, th, AF.Sin, scale=0.5)
        s.ts(tq, th, -1.0, ALU.mult)
        s.tt(tq, tq, th, ALU.max)
        s.ts(tq, tq, -0.5, ALU.mult, float(np.pi / 2), ALU.add)
        s.act(c2, tq, AF.Sin)
        s.tt(sn, s2, c2, ALU.mult)
        s.ts(sn, sn, 2.0, ALU.mult)
        s.tt(tq, s2, s2, ALU.mult)
        s.ts(tq, tq, -2.0, ALU.mult)
        s.tt(nr, rho, tq, ALU.mult)
        s.tt(nr, nr, em1, ALU.add)
        s.ts(L2[0], nr, 1.0, ALU.add)
        s.tt(L2[1], rho, sn, ALU.mult)
        s.tt(tq, lamr, lamr, ALU.mult)
        s.tt(tq2, lami, lami, ALU.mult)
        s.tt(tq, tq, tq2, ALU.add)
        s.recip(tq, tq)
        s.tt(betar, nr, lamr, ALU.mult)
        s.tt(tq2, L2[1], lami, ALU.mult)
        s.tt(betar, betar, tq2, ALU.add)
        s.tt(betar, betar, tq, ALU.mult)
        s.tt(betai, L2[1], lamr, ALU.mult)
        s.tt(tq2, nr, lami, ALU.mult)
        s.tt(betai, betai, tq2, ALU.subtract)
        s.tt(betai, betai, tq, ALU.mult)
        E2 = A(F32, [9, 2, 128])
        N2 = A(F32, [8, 2, 128])
        AP2 = A(F32, [8, 2, 128])
        t1s = A(F32, [2, 128]); t2s = A(F32, [2, 128])

        def cmul_small(out2, x2, Y2):
            s.cmul(out2, x2, Y2[0], Y2[1], t1s, t2s)

        s.memset(E2[0, 0], 1.0)
        s.memset(E2[0, 1], 0.0)
        s.cp(E2[1], L2)
        for e in range(2, 9):
            cmul_small(E2[e], E2[e - 1], L2)
        N1 = N2[1]
        s.tt(tq, L2[0], L2[0], ALU.mult)
        s.tt(tq2, L2[1], L2[1], ALU.mult)
        s.tt(tq, tq, tq2, ALU.add)
        s.recip(tq, tq)
        s.tt(N1[0], L2[0], tq, ALU.mult)
        s.tt(N1[1], L2[1], tq, ALU.mult)
        s.ts(N1[1], N1[1], -1.0, ALU.mult)
        s.memset(N2[0, 0], 1.0)
        s.memset(N2[0, 1], 0.0)
        for e in range(2, 8):
            cmul_small(N2[e], N2[e - 1], N2[1])
        s.cp(AP2[0], E2[8])
        for k in range(1, 8):
            cmul_small(AP2[k], AP2[k - 1], AP2[k - 1])
        s.cp(s.PT[0], AP2[7, 0])
        s.ts(s.PT[1], AP2[7, 1], -1.0, ALU.mult)
        s.cp(s.PT[2], AP2[7, 1])
        s.tap("E2", E2)
        s.tap("N2", N2)
        s.tap("AP2", AP2)
        s.tap("beta2", beta2)
        Dm = A(F32, [128])
        s.dma(Dm, s.d_s5Dm)
        def cmulE(out4, x2, Et, Etb, ne, gs_, T1, T2):
            Er = Et[0:ne, 0, gs_].T(1, 0).bc(2, 16)
            Ei = Et[0:ne, 1, gs_].T(1, 0).bc(2, 16)
            xr, xi = x2[0].bc(1, ne), x2[1].bc(1, ne)
            t1, t2 = T1[0, :, 0:ne, :], T2[0, :, 0:ne, :]
            s.tt(t1, xr, Er, ALU.mult)
            s.tt(t2, xi, Ei, ALU.mult)
            s.tt(out4[0], t1, t2, ALU.subtract)
            s.tt(t1, xr, Ei, ALU.mult)
            s.tt(t2, xi, Er, ALU.mult)
            s.tt(out4[1], t1, t2, ALU.add)

        E2b = N2b = None
        GB = 8
        BC = [(A(F32, [2, GB, 16]), A(F32, [2, GB, 16])) for _ in range(2)]
        s.dma(BC[0][0], s.d_s5B[:, 0:GB, :])
        s.dma(BC[0][1], s.d_s5C[:, 0:GB, :])
        mark = b.cur
        yield
        for i in range(128 // GB):
            b.cur = mark
            g0 = i * GB
            gs_ = slice(g0, g0 + GB)
            B2, C2 = BC[i % 2]
            if i + 1 < 128 // GB:
                gn = slice(g0 + GB, g0 + 2 * GB)
                s.dma(BC[(i + 1) % 2][0], s.d_s5B[:, gn, :])
                s.dma(BC[(i + 1) % 2][1], s.d_s5C[:, gn, :])
            G2 = A(F32, [2, GB, 16]); XC2 = A(F32, [2, GB, 16])
            T1 = A(F32, [2, GB, 9, 16]); T2 = A(F32, [2, GB, 9, 16])
            bcK = lambda v: v.bc(1, 16)
            s.cmul(G2, B2, bcK(betar[gs_]), bcK(betai[gs_]), T1[:, :, 0, :], T2[:, :, 0, :])
            s.cp(XC2.P(0, 64), G2.P(0, 64), eng="scalar")
            s.cp(XC2.P(64, 64), C2.P(64, 64), eng="scalar")
            Rw = A(F32, [2, GB, 8, 16])
            RQ = A(F32, [2, GB, 8, 16])
            Qp = A(F32, [2, GB, 9, 16])
            cmulE(Rw, G2, E2, E2b, 8, gs_, T1, T2)
            cmulE(RQ, XC2, N2, N2b, 8, gs_, T1, T2)
            cmulE(Qp, C2, E2, E2b, 9, gs_, T1, T2)
            Rn, Qn = RQ, RQ
            s.ts(Qp[1], Qp[1], -1.0, ALU.mult)
            s.ts(Qn[1].P(64, 64), Qn[1].P(64, 64), -1.0, ALU.mult)
            RwS = T2._new(dims=[(GB * 9 * 16, 2), (128, GB), (16, 8), (1, 16)])
            for comp in range(2):
                s.cp(RwS[comp].P(0, 64), Rw[comp, :, ::-1, :].P(0, 64))
                s.cp(RwS[comp].P(64, 64), Rw[comp].P(64, 64), eng="scalar")
            m1 = T1._new(dims=[(128, 4), (1, 128)])
            m2 = T1._new(off=T1.off + 512, dims=[(128, 4), (1, 128)])
            WT = [A(BF16, [8, 128]), A(BF16, [8, 128])]
            Mt = A(BF16, [8, 128]); Ot = [A(BF16, [8, 128]), A(BF16, [8, 128])]
            AT = A(F32, [8, 3, 8])
            for comp in range(2):
                pb = s.ps_alloc(2)
                pt = s.psv(pb, F32, [8, 128])
                for g in range(8):
                    s.trp(pt[g], RwS[comp, g].merge(), s.identf)
                s.cp(WT[comp][0:4], pt[0:4], eng="scalar")
                s.cp(WT[comp][4:8], pt[4:8], eng="scalar")
            for comp in range(2):
                ov = Ot[comp]._new(dims=[(128, 8), (16, 8), (1, 16)])
                s.cp(ov.P(0, 64), Qp[comp, :, 1:9, :].P(0, 64), eng="scalar")
                s.cp(ov.P(64, 64), Qp[comp, :, 8:0:-1, :].P(64, 64))
            for q4 in range(2):
                pf = s.ps_alloc(1)
                pbk = s.ps_alloc(1)
                Pf = s.psv(pf, F32, [4, 128]); Pb = s.psv(pbk, F32, [4, 128])
                for gg in range(4):
                    g = q4 * 4 + gg
                    s.mm(Pf[gg], Rn[0, g].merge().P(0, 64), Qp[0, g, 0:8, :].merge().P(0, 64), True, False)
                    s.mm(Pf[gg], Rn[1, g].merge().P(0, 64), Qp[1, g, 0:8, :].merge().P(0, 64), False, True)
                    s.mm(Pb[gg], Rw[0, g].merge().P(64, 64), Qn[0, g].merge().P(64, 64), True, False)
                    s.mm(Pb[gg], Rw[1, g].merge().P(64, 64), Qn[1, g].merge().P(64, 64), False, True)
                s.tt(m1, Pf, s.maskF.bc(0, 4), ALU.mult)
                s.tt(m2, Pb, s.maskB.bc(0, 4), ALU.mult)
                s.tt(m1, m1, m2, ALU.add)
                gsl = slice(g0 + q4 * 4, g0 + q4 * 4 + 4)
                s.tt(m2, s.identf.bc(0, 4), Dm[gsl].bc(1, 128), ALU.mult)
                s.tt(Mt[q4 * 4:q4 * 4 + 4], m1, m2, ALU.add)
            s.cp(AT[:, 0, :], AP2[:, 0, gs_])
            s.ts(AT[:, 1, :], AP2[:, 1, gs_], -1.0, ALU.mult)
            s.cp(AT[:, 2, :], AP2[:, 1, gs_])
            for nm, tv in (("WTr", WT[0]), ("WTi", WT[1]), ("M", Mt), ("Or", Ot[0]), ("Oi", Ot[1])):
                s.dma(s.dsel(s.d_tab[nm], None, i, dkey=i), tv.merge())
            s.dma(s.dsel(s.d_tabAT, None, i, dkey=i), AT.merge())
            yield
        if "tabs" in s.dbg:
            for nm in ("WTr", "WTi", "M", "Or", "Oi"):
                d = s.dram("dbg_tab_" + nm, [16, 128, 1024], BF16, kind="ExternalOutput")
                s.taps.append(("dbg_tab_" + nm, [16, 128, 1024]))
                for i in range(16):
                    s.dma(s.dsel(d, None, i, dkey=i), s.dsel(s.d_tab[nm], None, i, dkey=i))
            d = s.dram("dbg_tab_AT", [16, 128, 192], F32, kind="ExternalOutput")
            s.taps.append(("dbg_tab_AT", [16, 128, 192]))
            for i in range(16):
                s.dma(s.dsel(d, None, i, dkey=i), s.dsel(s.d_tabAT, None, i, dkey=i))


def _fm(v):
    v = np.asarray(v, np.float32).reshape(-1, 128)
    return np.ascontiguousarray(v.T)


def _slabs(W, ncols_per=128):
    Kd, N = W.shape
    a = W.reshape(Kd // 128, 128, N // ncols_per, ncols_per)
    return np.ascontiguousarray(a.transpose(2, 1, 0, 3))


def _xT(x):
    T = x.shape[0]
    a = x.reshape(T // 8, 8, NDK, 128).transpose(3, 2, 1, 0)
    return np.ascontiguousarray(a.reshape(128, NDK, T))


def _unT(yT):
    T = yT.shape[2]
    a = np.asarray(yT).reshape(128, NDK, 8, T // 8).transpose(3, 2, 1, 0)
    return np.ascontiguousarray(a.reshape(T, D))


def prep_shared(inp):
    sh = {}
    sh["w_ada_s"] = _slabs(inp["w_ada"][0], 512)
    sh["w_in_s"] = _slabs(inp["w_in"][0], 128)
    sh["w_glu_s"] = _slabs(inp["s5_w_glu"][0], 128)
    sh["w_os_s"] = _slabs(inp["w_out_s5"][0], 128)
    sh["w_ol_s"] = _slabs(inp["w_out_lru"][0], 128)
    sh["w_o_s"] = _slabs(inp["w_out"][0], 128)
    wr, wi = inp["lru_w_r"][0], inp["lru_w_i"][0]
    wg = np.stack([wr[0], wi[0], wr[1], wi[1]], axis=0)
    sh["w_gate"] = np.ascontiguousarray(wg.transpose(1, 2, 0, 3))
    lr, li, ls = inp["s5_lam_re"][0], inp["s5_lam_im"][0], inp["s5_log_step"][0]
    pk = lambda a: np.ascontiguousarray(a.transpose(0, 2, 1).reshape(128, 128))
    lsb = np.broadcast_to(ls[:, None, :], (2, 64, 128)).reshape(128, 128)
    sh["s5small"] = np.ascontiguousarray(np.stack([pk(lr), pk(li), lsb], axis=1).astype(np.float32))
    br, bi = inp["s5_b_re"][0], inp["s5_b_im"][0]
    pb = lambda a: a.transpose(0, 2, 1, 3).reshape(128, 128, 16)
    sh["s5B"] = np.ascontiguousarray(np.stack([pb(br), pb(bi)], axis=1))
    cr, ci = inp["s5_c_re"][0], inp["s5_c_im"][0]
    pc = lambda a: a.transpose(0, 3, 1, 2).reshape(128, 128, 16)
    sh["s5C"] = np.ascontiguousarray(np.stack([pc(cr), pc(ci)], axis=1))
    d = inp["s5_d"][0].reshape(128, 16)
    sh["s5Dm"] = np.ascontiguousarray(np.broadcast_to(d.T[None, :, :], (8, 16, 128)).reshape(128, 128))
    ident = np.eye(128, dtype=np.float32)
    sidx = np.arange(128) // 16
    maskF = (sidx[None, :] >= sidx[:, None]).astype(np.float32)
    maskB = (sidx[None, :] <= sidx[:, None]).astype(np.float32)
    sh["consts"] = np.ascontiguousarray(np.stack([ident, maskF, maskB], axis=1))
    return sh


def prep_core(inp, r):
    b, q = r // 4, r % 4
    m = {}
    xs = inp["x_sample"][b]
    m["xT_own"] = _xT(xs[q * 1024:(q + 1) * 1024])
    others = [j for j in range(4) if j != q]
    m["xT_oth"] = np.stack([_xT(xs[j * 1024:(j + 1) * 1024]) for j in others], axis=0)
    m["xT_p"] = _xT(inp["x_prompt"][2 * r:2 * r + 2].reshape(512, D))
    m["condT"] = np.ascontiguousarray(np.stack([_fm(inp["c"][b]), _fm(inp["c_ctx"])], axis=2))
    mf = np.array([1.0 if j < q else 0.0 for j in others], np.float32)
    mb = np.array([1.0 if j > q else 0.0 for j in others], np.float32)
    fm2 = lambda a: np.concatenate([_fm(a[0]), _fm(a[1])], axis=1)
    vec = np.concatenate([
        _fm(inp["b_ada"][0]), _fm(inp["norm_g"][0]), _fm(inp["final_g"]), _fm(inp["s5_b_glu"][0]),
        np.concatenate([_fm(inp["lru_conv_w"][0][k]) for k in range(4)], axis=1),
        _fm(inp["lru_conv_b"][0]), fm2(inp["lru_b_r"][0]), fm2(inp["lru_b_i"][0]), fm2(inp["lru_lam"][0]),
        fm2(inp["state_lru"][b, 0]),
        np.broadcast_to(np.concatenate([mf, mb])[None, :], (128, 6)),
    ], axis=1).astype(np.float32)
    assert vec.shape == (128, 390), vec.shape
    m["vecs"] = np.ascontiguousarray(np.pad(vec, ((0, 0), (0, 2))))
    sr, si = inp["state_s5_re"][b, 0], inp["state_s5_im"][b, 0]
    pk = lambda a: a.transpose(0, 2, 1).reshape(128, 128)
    m["s5h0"] = np.ascontiguousarray(np.stack([pk(sr), pk(si)], axis=1))
    return m


_CACHE = {}


def kernel(**inputs):
    inp = {k: np.asarray(v) for k, v in inputs.items()}
    if "nc" not in _CACHE:
        _CACHE["nc"] = K().build()
    nc = _CACHE["nc"]
    sh = prep_shared(inp)
    in_maps = []
    for r in range(NCORES):
        m = dict(sh)
        m.update(prep_core(inp, r))
        in_maps.append(m)
    res = run_bass_kernel_spmd(nc, in_maps, core_ids=list(range(NCORES)))
    return assemble(res.results)


def assemble(results):
    y_prompt = np.zeros((16, 256, D), np.float32)
    y_sample = np.zeros((2, 4096, D), np.float32)
    s_re = np.zeros((16, 1, 2, 128, 64), np.float32)
    s_im = np.zeros((16, 1, 2, 128, 64), np.float32)
    s_lru = np.zeros((16, 1, 2, 2048), np.float32)
    for r in range(NCORES):
        b, q = r // 4, r % 4
        o = results[r]
        ys = _unT(o["yT_s"])
        y_sample[b, q * 1024:(q + 1) * 1024] = ys
        yp = _unT(o["yT_p"]).reshape(2, 256, D)
        y_prompt[2 * r:2 * r + 2] = yp
        sl = np.asarray(o["st_lru"])
        s_lru[2 * r:2 * r + 2, 0] = sl.transpose(2, 3, 1, 0).reshape(2, 2, 2048)
        s5 = np.asarray(o["st_s5"]).reshape(2, 64, 2, 2, 128)
        s_re[2 * r:2 * r + 2, 0] = s5[:, :, :, 0, :].transpose(2, 0, 3, 1)
        s_im[2 * r:2 * r + 2, 0] = s5[:, :, :, 1, :].transpose(2, 0, 3, 1)
    return (y_prompt, y_sample, s_re, s_im, s_lru)
```
